# Optimizing a Trainium2 kernel written in Bass

```python
import jax, jax.numpy as jnp
from jax import lax
import numpy as np

D_MODEL = 4096
BATCH = 2
SEQ = 4096
DEPTH = 1
DEC_BATCH = 2
DEC_SEQ = 8192
PAST_LEN = 128

HEAD_DIM = 128
ROPE_THETA = 10000.0
NORM_EPS = 1e-6
NEG_INF = -1e30

DIL_CONFIGS = ((128, 1), (512, 4), (2048, 16))
N_DIL_GROUPS = 3
A_HEADS = 8
A_WIDTH = A_HEADS * HEAD_DIM
A_QKV_WIDTH = N_DIL_GROUPS * A_WIDTH

B_Q_HEADS = 16
B_KV_HEADS = 4
B_RADIUS = 128
B_BLOCK = 128
B_Q_WIDTH = B_Q_HEADS * HEAD_DIM
B_KV_WIDTH = B_KV_HEADS * HEAD_DIM

MEM_LEN = 256
M_HEADS = 4
M_HEAD_DIM = 256
M_WIDTH = M_HEADS * M_HEAD_DIM

MIX_WIDTH = A_WIDTH + B_Q_WIDTH + M_WIDTH
N_BRANCH = 3
IN_WIDTH = 3 * A_QKV_WIDTH + B_Q_WIDTH + 2 * B_KV_WIDTH + M_WIDTH + MIX_WIDTH + N_BRANCH * D_MODEL

kernel_name = "hybrid_dilated_window_memory_encoder"


def _col_slices():
    sizes = (("a_qkv", 3 * A_QKV_WIDTH), ("b_q", B_Q_WIDTH), ("b_kv", 2 * B_KV_WIDTH), ("m_q", M_WIDTH),
             ("z_a", A_WIDTH), ("z_b", B_Q_WIDTH), ("z_m", M_WIDTH),
             ("g_a", D_MODEL), ("g_b", D_MODEL), ("g_m", D_MODEL))
    out = {}
    lo = 0
    for name, n in sizes:
        out[name] = (lo, lo + n)
        lo += n
    return out


def _branch_rows():
    return {"a": (0, A_WIDTH), "b": (A_WIDTH, A_WIDTH + B_Q_WIDTH), "m": (A_WIDTH + B_Q_WIDTH, MIX_WIDTH)}


def _proj(h, w_in, name):
    lo, hi = _col_slices()[name]
    return jnp.einsum('bsd,de->bse', h, w_in[:, lo:hi])


def _rmsnorm(x, g):
    x32 = x.astype(jnp.float32)
    y = x32 * lax.rsqrt(jnp.mean(x32 * x32, axis=-1, keepdims=True) + NORM_EPS)
    return y.astype(x.dtype) * g


def _rope(x):
    s, dh = x.shape[1], x.shape[-1]
    inv_freq = ROPE_THETA ** (-jnp.arange(0, dh, 2, dtype=jnp.float32) / dh)
    ang = jnp.arange(s, dtype=jnp.float32)[:, None] * inv_freq[None, :]
    shp = (s,) + (1,) * (x.ndim - 3) + (dh // 2,)
    cos = jnp.cos(ang).reshape(shp).astype(x.dtype)
    sin = jnp.sin(ang).reshape(shp).astype(x.dtype)
    x1, x2 = x[..., : dh // 2], x[..., dh // 2:]
    return jnp.concatenate([x1 * cos - x2 * sin, x2 * cos + x1 * sin], axis=-1)


def _banded_attention(q, k, v, radius, blk, sink=None):
    n, L, H, dh = q.shape
    hkv = k.shape[2]
    g = H // hkv
    nb = -(-L // blk)
    lp = nb * blk
    pad = lp - L
    qb = jnp.pad(q, ((0, 0), (0, pad), (0, 0), (0, 0))).reshape(n, nb, blk, hkv, g, dh)

    def windows(t):
        tp = jnp.pad(t, ((0, 0), (blk, pad + blk), (0, 0), (0, 0))).reshape(n, nb + 2, blk, hkv, dh)
        return jnp.concatenate([tp[:, :-2], tp[:, 1:-1], tp[:, 2:]], axis=2)

    kw, vw = windows(k), windows(v)
    s = jnp.einsum('nbqhgd,nbkhd->nbhgqk', qb, kw, preferred_element_type=jnp.float32) * (dh ** -0.5)
    qpos = jnp.arange(nb)[:, None] * blk + jnp.arange(blk)[None, :]
    kpos = jnp.arange(nb)[:, None] * blk - blk + jnp.arange(3 * blk)[None, :]
    valid = ((jnp.abs(qpos[:, :, None] - kpos[:, None, :]) <= radius)
             & (kpos[:, None, :] >= 0) & (kpos[:, None, :] < L))
    s = jnp.where(valid[None, :, None, None], s, NEG_INF)
    m = jnp.max(s, axis=-1, keepdims=True)
    if sink is not None:
        sk = sink.astype(jnp.float32).reshape(1, 1, hkv, g, 1, 1)
        m = jnp.maximum(m, sk)
    e = jnp.exp(s - m)
    den = jnp.sum(e, axis=-1, keepdims=True)
    if sink is not None:
        den = den + jnp.exp(sk - m)
    lse = (m + jnp.log(den))[..., 0]
    p = (e / den).astype(v.dtype)
    o = jnp.einsum('nbhgqk,nbkhd->nbqhgd', p, vw).reshape(n, lp, H, dh)[:, :L]
    lse = lse.transpose(0, 1, 4, 2, 3).reshape(n, lp, H)[:, :L]
    return o, lse


def _dilated_group(q, k, v, window, dilation):
    b, s, h, dh = q.shape
    radius = window // (2 * dilation)
    sd = s // dilation

    def split(t):
        return t.reshape(b, sd, dilation, h, dh).transpose(0, 2, 1, 3, 4).reshape(b * dilation, sd, h, dh)

    o, lse = _banded_attention(split(q), split(k), split(v), radius, radius)
    o = o.reshape(b, dilation, sd, h, dh).transpose(0, 2, 1, 3, 4).reshape(b, s, h, dh)
    lse = lse.reshape(b, dilation, sd, h).transpose(0, 2, 1, 3).reshape(b, s, h)
    return o, lse


def _mixer_a(h, w_in):
    b, s, _ = h.shape
    qkv = _proj(h, w_in, "a_qkv").reshape(b, s, 3, N_DIL_GROUPS, A_HEADS, HEAD_DIM)
    q, k, v = _rope(qkv[:, :, 0]), _rope(qkv[:, :, 1]), qkv[:, :, 2]
    outs, lses = [], []
    for gi, (window, dilation) in enumerate(DIL_CONFIGS):
        o, lse = _dilated_group(q[:, :, gi], k[:, :, gi], v[:, :, gi], window, dilation)
        outs.append(o)
        lses.append(lse)
    alpha = jax.nn.softmax(jnp.stack(lses, axis=0), axis=0).astype(h.dtype)
    o = alpha[0][..., None] * outs[0] + alpha[1][..., None] * outs[1] + alpha[2][..., None] * outs[2]
    return o.reshape(b, s, A_WIDTH)


def _mixer_b(h, w_in, sink):
    b, s, _ = h.shape
    q = _rope(_proj(h, w_in, "b_q").reshape(b, s, B_Q_HEADS, HEAD_DIM))
    kv = _proj(h, w_in, "b_kv").reshape(b, s, 2, B_KV_HEADS, HEAD_DIM)
    k, v = _rope(kv[:, :, 0]), kv[:, :, 1]
    o, _ = _banded_attention(q, k, v, B_RADIUS, B_BLOCK, sink)
    return o.reshape(b, s, B_Q_WIDTH)


def _memory_attn(h, mem, w_in, g_mem, w_mem_kv):
    b, s, _ = h.shape
    q = _proj(h, w_in, "m_q").reshape(b, s, M_HEADS, M_HEAD_DIM)
    kv = jnp.einsum('bmd,de->bme', _rmsnorm(mem, g_mem), w_mem_kv).reshape(b, MEM_LEN, 2, M_HEADS, M_HEAD_DIM)
    sc = jnp.einsum('bshd,bmhd->bhsm', q, kv[:, :, 0], preferred_element_type=jnp.float32) * (M_HEAD_DIM ** -0.5)
    p = jax.nn.softmax(sc, axis=-1).astype(h.dtype)
    o = jnp.einsum('bhsm,bmhd->bshd', p, kv[:, :, 1])
    return o.reshape(b, s, M_WIDTH)


def _layer(x, mem, g_norm, w_in, sink, g_mem, w_mem_kv, w_branch, w_out):
    h = _rmsnorm(x, g_norm)
    branch_out = {"a": _mixer_a(h, w_in), "b": _mixer_b(h, w_in, sink),
                  "m": _memory_attn(h, mem, w_in, g_mem, w_mem_kv)}
    rows = _branch_rows()
    u = None
    for name in ("a", "b", "m"):
        z = jax.nn.silu(_proj(h, w_in, "z_" + name))
        gate = jax.nn.sigmoid(_proj(h, w_in, "g_" + name))
        lo, hi = rows[name]
        term = gate * jnp.einsum('bse,ed->bsd', branch_out[name] * z, w_branch[lo:hi])
        u = term if u is None else u + term
    return x + jnp.einsum('bsd,de->bse', u, w_out)


def _trunk(x, mem, g_norm, w_in, attn_sink, g_mem, w_mem_kv, w_branch, w_out, g_final):
    for l in range(DEPTH):
        x = _layer(x, mem, g_norm[l], w_in[l], attn_sink[l], g_mem[l], w_mem_kv[l], w_branch[l], w_out[l])
    return _rmsnorm(x, g_final)


def setup_inputs(seed: int = 0) -> dict:
    key = jax.random.key(seed)
    ks = jax.random.split(key, 12)
    f32 = jnp.float32
    return {
        "x_prompt": jax.random.normal(ks[0], (BATCH, SEQ, D_MODEL), f32),
        "x_sample": jax.random.normal(ks[1], (DEC_BATCH, DEC_SEQ, D_MODEL), f32),
        "mem_prompt": jax.random.normal(ks[2], (BATCH, MEM_LEN, D_MODEL), f32),
        "mem_sample": jax.random.normal(ks[3], (DEC_BATCH, MEM_LEN, D_MODEL), f32),
        "g_norm": 1.0 + 0.02 * jax.random.normal(ks[4], (DEPTH, D_MODEL), f32),
        "w_in": jax.random.normal(ks[5], (DEPTH, D_MODEL, IN_WIDTH), f32) * (D_MODEL ** -0.5),
        "attn_sink": 0.5 * jax.random.normal(ks[6], (DEPTH, B_Q_HEADS), f32),
        "g_mem": 1.0 + 0.02 * jax.random.normal(ks[7], (DEPTH, D_MODEL), f32),
        "w_mem_kv": jax.random.normal(ks[8], (DEPTH, D_MODEL, 2 * M_WIDTH), f32) * (D_MODEL ** -0.5),
        "w_branch": jax.random.normal(ks[9], (DEPTH, MIX_WIDTH, D_MODEL), f32) * (MIX_WIDTH ** -0.5),
        "w_out": jax.random.normal(ks[10], (DEPTH, D_MODEL, D_MODEL), f32) * (D_MODEL ** -0.5),
        "g_final": 1.0 + 0.02 * jax.random.normal(ks[11], (D_MODEL,), f32),
    }


def reference(x_prompt, x_sample, mem_prompt, mem_sample, g_norm, w_in, attn_sink, g_mem, w_mem_kv, w_branch, w_out, g_final):
    y_prompt = _trunk(x_prompt, mem_prompt, g_norm, w_in, attn_sink, g_mem, w_mem_kv, w_branch, w_out, g_final)
    y_sample = _trunk(x_sample, mem_sample, g_norm, w_in, attn_sink, g_mem, w_mem_kv, w_branch, w_out, g_final)
    return (y_prompt, y_sample)
```

```python
import numpy as np
import concourse.bass as bass
import concourse.mybir as mybir
from concourse.bass_utils import run_bass_kernel_spmd
from contextlib import ExitStack

F32 = mybir.dt.float32
BF16 = mybir.dt.bfloat16
AF = mybir.ActivationFunctionType
ALU = mybir.AluOpType
ENGS = ("pe", "act", "dve", "pool", "sp")


class Op:
    __slots__ = ("eng", "fn", "deps", "is_dma", "sig", "token", "n_dma", "idx")

    def __init__(self, eng, fn, is_dma=False, n_dma=1):
        self.eng = eng
        self.fn = fn
        self.deps = ()
        self.is_dma = is_dma
        self.sig = False
        self.token = None
        self.n_dma = n_dma
        self.idx = -1


class Prog:
    def __init__(self, nc, stack):
        self.nc = nc
        self.stack = stack
        self.ops = {e: [] for e in ENGS}
        self.last_w = {}
        self.readers = {}
        self.dry = False
        self.eng_sem = {e: stack.enter_context(nc.semaphore("s_" + e)) for e in ENGS}
        self.dma_sems = {}
        self.dma_cnt = {}
        self.final_tokens = []
        self.last_dma = {}

    def _track(self, o, reads, writes):
        deps = []
        seen = set()

        def add(d):
            if d is not None and d is not o and id(d) not in seen:
                seen.add(id(d))
                deps.append(d)

        for k in reads:
            add(self.last_w.get(k))
        for k in writes:
            add(self.last_w.get(k))
            for r in self.readers.get(k, ()):
                add(r)
        o.deps = deps
        for k in reads:
            self.readers.setdefault(k, []).append(o)
        for k in writes:
            self.last_w[k] = o
            self.readers[k] = []

    def op(self, eng, fn, reads=(), writes=()):
        if self.dry:
            return None
        ps_r = [k for k in reads if isinstance(k, tuple) and k[0] == "ps"]
        if ps_r:
            reads = [k for k in reads if not (isinstance(k, tuple) and k[0] == "ps")]
            writes = list(writes) + ps_r
        o = Op(eng, fn)
        self._track(o, reads, writes)
        o.idx = len(self.ops[eng])
        self.ops[eng].append(o)
        return o

    def dma(self, eng, fn, semkey, reads=(), writes=(), n=1, final=False):
        if self.dry:
            return None
        o = Op(eng, fn, is_dma=True, n_dma=n)
        if semkey not in self.dma_sems:
            self.dma_sems[semkey] = self.stack.enter_context(self.nc.semaphore("d_%d" % len(self.dma_sems)))
            self.dma_cnt[semkey] = 0
        self.dma_cnt[semkey] += 16 * n
        o.token = (self.dma_sems[semkey], self.dma_cnt[semkey])
        self._track(o, reads, writes)
        prev = self.last_dma.get(semkey)
        if prev is not None and all(prev is not d for d in o.deps):
            o.deps.append(prev)
        self.last_dma[semkey] = o
        o.idx = len(self.ops[eng])
        self.ops[eng].append(o)
        if final:
            self.final_tokens.append(o.token)
        return o

    @staticmethod
    def _need_wait(o, d):
        if d.is_dma:
            return True
        if d.eng == o.eng and o.eng == "pe" and not o.is_dma:
            return False
        return True

    def _reduce(self, o):
        comp = {}
        dmas = {}
        for d in o.deps:
            if not self._need_wait(o, d):
                continue
            if d.is_dma:
                sem, val = d.token
                if sem.num not in dmas or dmas[sem.num][1] < val:
                    dmas[sem.num] = (sem, val)
            else:
                if d.eng not in comp or comp[d.eng].idx < d.idx:
                    comp[d.eng] = d
        return comp, dmas

    def emit(self):
        nc = self.nc
        for e in ENGS:
            for o in self.ops[e]:
                comp, dmas = self._reduce(o)
                o.deps = (comp, dmas)
                for d in comp.values():
                    d.sig = True
        for e in ENGS:
            c = 0
            for o in self.ops[e]:
                if not o.is_dma and o.sig:
                    c += 1
                    o.token = (self.eng_sem[e], c)
        self.n_wait = {e: 0 for e in ENGS}
        self.n_ops = {e: len(self.ops[e]) for e in ENGS}

        def run(ename, eng):
            seen = {}

            def wait(sem, val):
                if seen.get(sem.num, 0) >= val:
                    return
                seen[sem.num] = val
                eng.wait_ge(sem, val)
                self.n_wait[ename] += 1

            for o in self.ops[ename]:
                comp, dmas = o.deps
                for d in comp.values():
                    wait(*d.token)
                for sem, val in dmas.values():
                    wait(sem, val)
                r = o.fn(eng)
                if o.is_dma:
                    if not isinstance(r, (list, tuple)):
                        r = [r]
                    assert len(r) == o.n_dma, (len(r), o.n_dma)
                    for ins in r:
                        ins.then_inc(o.token[0], 16)
                elif o.sig:
                    r.then_inc(o.token[0], 1)
            if ename == "sp":
                for sem, val in self.final_tokens:
                    wait(sem, val)

        with nc.Block() as block:
            @block.tensor
            def _(eng):
                run("pe", eng)

            @block.scalar
            def _(eng):
                run("act", eng)

            @block.vector
            def _(eng):
                run("dve", eng)

            @block.gpsimd
            def _(eng):
                run("pool", eng)

            @block.sync
            def _(eng):
                run("sp", eng)


NCORES = 8
HD = 128
MAXH = 1024
TQ = 256
SEG_CORE = {"P": 1024, "S": 2048}
SEG_LEN = {"P": 4096, "S": 8192}
GRP = {"n": (1, 128), "d4": (4, 256), "d16": (16, 1024)}
MIXC = 32
NVAL = 47
DEBUG_STOP = 100
DEBUG_QT = 100
DEBUG_KV = 100
DEBUG_ROPE = 100
DEBUG_V1 = 0


def in_col_blocks(D):
    o_bq = 9216
    o_bkv = o_bq + 2048
    o_mq = o_bkv + 1024
    o_za = o_mq + 1024
    o_zb = o_za + 1024
    o_zm = o_zb + 2048
    o_ga = o_zm + 1024
    o_gb = o_ga + D
    o_gm = o_gb + D
    ar = np.arange(128)

    def aq(t, g, h):
        return ((t * 3 + g) * 8 + h) * 128 + ar

    blocks = {}
    for g in range(3):
        for h in range(8):
            blocks[("ak", g, h)] = aq(1, g, h)
        for h in range(8):
            blocks[("av", g, h)] = aq(2, g, h)
    for h in range(4):
        blocks[("bk", h)] = o_bkv + h * 128 + ar
    for h in range(4):
        blocks[("bv", h)] = o_bkv + (4 + h) * 128 + ar
    for s in range(8):
        for g in range(3):
            blocks[("qa", s, g)] = aq(0, g, s)
        blocks[("za", s)] = o_za + s * 128 + ar
    for h in range(16):
        blocks[("qb", h)] = o_bq + h * 128 + ar
        blocks[("zb", h)] = o_zb + h * 128 + ar
    for h in range(4):
        for c in range(2):
            blocks[("mq", h, c)] = o_mq + h * 256 + c * 128 + ar
            blocks[("zm", h, c)] = o_zm + h * 256 + c * 128 + ar
    for dc in range(D // 128):
        blocks[("ga", dc)] = o_ga + dc * 128 + ar
        blocks[("gb", dc)] = o_gb + dc * 128 + ar
        blocks[("gm", dc)] = o_gm + dc * 128 + ar
    return blocks


def kv_tiles(seg, grp):
    d, H = GRP[grp]
    L = SEG_CORE[seg] + 2 * H
    out = []
    i = 0
    while i < L:
        n = min(512, L - i)
        out.append((i, n))
        i += n
    return d, H, L, L // d, out


def build_program(D):
    KC = D // 128
    WS = max(KC, MIXC) * 128
    G4 = min(4, KC)
    blocks = in_col_blocks(D)
    bnames = list(blocks.keys())
    bidx = {n: i for i, n in enumerate(bnames)}
    NBIN = len(bnames)

    nc = bass.Bass("TRN2", target_bir_lowering=False)
    dt_in = lambda name, shape: nc.dram_tensor(name, list(shape), F32, kind="ExternalInput").ap()
    X = {s: dt_in("x" + s, [SEG_CORE[s] + 2 * MAXH + 16, D]) for s in "PS"}
    MEM = {s: dt_in("mem" + s, [256, D]) for s in "PS"}
    GN = dt_in("gn", [1, D])
    GM = dt_in("gm", [1, D])
    GF = dt_in("gf", [1, D])
    SINK = dt_in("sink", [1, 16])
    WIN = dt_in("win", [NBIN, 128, KC * 128])
    WMEM = dt_in("wmem", [16, 128, KC * 128])
    WBR = dt_in("wbr", [KC, 128, MIXC * 128])
    WOUT = dt_in("wout", [KC, 128, KC * 128])
    COSQ = {s: dt_in("cosq" + s, [128, SEG_CORE[s]]) for s in "PS"}
    SINQ = {s: dt_in("sinq" + s, [128, SEG_CORE[s]]) for s in "PS"}
    COSK = {}
    SINK_ = {}
    for s in "PS":
        for g in GRP:
            L = SEG_CORE[s] + 2 * GRP[g][1]
            COSK[s, g] = dt_in("cosk%s%s" % (s, g), [128, L])
            SINK_[s, g] = dt_in("sink%s%s" % (s, g), [128, L])
    VALW = {s: dt_in("valw" + s, [SEG_CORE[s] // TQ, 128, NVAL]) for s in "PS"}
    CONST = dt_in("const", [128, 3 * 128 + 6 * 256])
    Y = {s: nc.dram_tensor("y" + s, [SEG_CORE[s], D], F32, kind="ExternalOutput").ap() for s in "PS"}

    dint = lambda name, shape: nc.dram_tensor(name, list(shape), BF16, kind="Internal").ap()
    WSCR = {"win": dint("s_win", [NBIN, 128, KC * 128]), "wmem": dint("s_wmem", [16, 128, KC * 128]),
            "wbr": dint("s_wbr", [KC, 128, MIXC * 128]), "wout": dint("s_wout", [KC, 128, KC * 128])}
    WSRC = {"win": WIN, "wmem": WMEM, "wbr": WBR, "wout": WOUT}
    KTC = {}
    VC = {}
    for s in "PS":
        for gi, g in enumerate(("n", "d4", "d16")):
            L = SEG_CORE[s] + 2 * GRP[g][1]
            KTC[s, "a", gi] = dint("kt_%s_a%d" % (s, gi), [8, 128, L])
            VC[s, "a", gi] = dint("v_%s_a%d" % (s, gi), [L, 1024])
        L = SEG_CORE[s] + 2 * GRP["n"][1]
        KTC[s, "b", 0] = dint("kt_%s_b" % s, [4, 128, L])
        VC[s, "b", 0] = dint("v_%s_b" % s, [L, 512])

    st = ExitStack()
    with st:
        P = Prog(nc, st)
        sbt = lambda name, shape, dt: st.enter_context(nc.sbuf_tensor(name, list(shape), dt))
        PS = [st.enter_context(nc.psum_tensor("psb%d" % i, [128, 512], F32)) for i in range(8)]
        BK_G = (0, 1)
        BK_G4 = (0, 1, 3, 4)
        BK_ROT = 2
        BK_S = (3, 4)
        BK_ACC = (5, 6, 7)
        cnt = {"g": 0, "s": 0, "g4": 0}

        def gbank():
            b = BK_G[cnt["g"] % 2]
            cnt["g"] += 1
            return b

        def gbank4():
            b = BK_G4[cnt["g4"] % 4]
            cnt["g4"] += 1
            return b

        def sbank():
            b = BK_S[cnt["s"] % 2]
            cnt["s"] += 1
            return b

        ACTB = sbt("actb", [128, KC * 256 + MIXC * 256 + KC * 256], BF16)
        hT = ACTB[:, 0:KC * 256].rearrange("p (k n) -> p k n", k=KC)
        bzT = ACTB[:, KC * 256:KC * 256 + MIXC * 256].rearrange("p (k n) -> p k n", k=MIXC)
        uT = ACTB[:, KC * 256 + MIXC * 256:].rearrange("p (k n) -> p k n", k=KC)
        hTkv = ACTB[:, 0:KC * 512].rearrange("p (k n) -> p k n", k=KC)
        xo = [sbt("xo%d" % i, [128, D], F32) for i in range(2)]
        hbs = [sbt("hb%d" % i, [128, D], BF16) for i in range(2)]
        stat2 = sbt("stat2", [128, 8], F32)
        fcnt = {"n": 0}
        gt = sbt("gt", [128, D], F32)
        NSLOT = 4
        WSL = sbt("wsl", [128, NSLOT, WS], BF16)
        cst = sbt("cst", [128, 3 * 128 + 6 * 256], BF16)
        ident = cst[:, 0:128]
        rotm = cst[:, 128:256]
        ones = cst[:, 256:384]
        masks = cst[:, 384:].rearrange("p (m n) -> p m n", m=6)
        sinkt = sbt("sinkt", [128, 16], F32)
        esink = sbt("esink", [128, 16], F32)
        stat = sbt("stat", [128, 16], F32)
        cosb = sbt("cosb", [128, 512], F32)
        sinb = sbt("sinb", [128, 512], F32)
        qbf = [sbt("qbf%d" % i, [128, 512], BF16) for i in range(2)]
        t1b = [sbt("t1b%d" % i, [128, 512], F32) for i in range(1)]
        t2b = [sbt("t2b%d" % i, [128, 512], F32) for i in range(1)]
        kout = [sbt("kout%d" % i, [128, 512], BF16) for i in range(2)]
        vout = [sbt("vout%d" % i, [128, 256], BF16) for i in range(2)]
        qT = [sbt("qT%d" % i, [128, 256], BF16) for i in range(8)]
        zs = [sbt("zs%d" % i, [128, 256], F32) for i in range(2)]
        Eb = [sbt("Eb%d" % i, [128, 512], BF16) for i in range(2)]
        PTb = [sbt("PTb%d" % i, [128, 512], BF16) for i in range(3)]
        rden = sbt("rden", [128, 256], F32)
        tnum = sbt("tnum", [128, 256], F32)
        tsum = sbt("tsum", [128, 256], F32)
        gs = [sbt("gs%d" % i, [128, 256], F32) for i in range(3)]
        uacc = sbt("uacc", [128, 256], F32)
        utmp = sbt("utmp", [128, 256], F32)
        valw = sbt("valw", [128, NVAL], BF16)
        KmT = sbt("KmT", [128, 8, 256], BF16)
        Vm = sbt("Vm", [128, 2, 1024], BF16)
        k1 = sbt("k1", [128, 384], BF16)
        k2 = sbt("k2", [128, 4, 192], BF16)
        k3 = sbt("k3", [128, 16, 144], BF16)
        v1 = sbt("v1", [128, 3, 128], BF16)
        v2lo = sbt("v2lo", [128, 4, 128], BF16)
        v2hi = sbt("v2hi", [128, 4, 128], BF16)
        v3lo = sbt("v3lo", [128, 16, 128], BF16)
        v3hi = sbt("v3hi", [128, 16, 128], BF16)
        kbw = sbt("kbw", [128, 512], BF16)
        vbw = sbt("vbw", [128, 4, 128], BF16)
        rr = {"q": 0, "t": 0, "k": 0, "v": 0, "qT": 0, "z": 0, "E": 0, "PT": 0}

        def rot(name, lst):
            i = rr[name] % len(lst)
            rr[name] += 1
            return i, lst[i]

        plan = []
        wstate = {"n": 0, "issued": 0, "cast": set(), "ncast": 0, "castptr": 0}

        def cast_ahead(mm):
            return min(160, 12 + mm // 2)

        def w_issue(m):
            mm = wstate["castptr"]
            while mm < len(plan) and mm - cast_ahead(mm) < m:
                if plan[mm] is not None:
                    f2, b2 = plan[mm]
                    if (f2, b2) not in wstate["cast"]:
                        wstate["cast"].add((f2, b2))
                        dep = mm - cast_ahead(mm)
                        while dep >= 0 and plan[dep] is None:
                            dep -= 1
                        P.dma("pool", lambda e, f2=f2, b2=b2: [e.dma_start(out=WSCR[f2][b2], in_=WSRC[f2][b2])],
                              ("wcast", wstate["ncast"] % 8), reads=([("wl", dep)] if dep >= 0 else []),
                              writes=[("wscr", f2, b2)])
                        wstate["ncast"] += 1
                mm += 1
            wstate["castptr"] = mm
            if plan[m] is None:
                return
            fam, blk = plan[m]
            assert (fam, blk) in wstate["cast"]
            slot = m % NSLOT
            n_el = (MIXC if fam == "wbr" else KC) * 128
            P.dma("sp", lambda e, fam=fam, blk=blk, slot=slot, n_el=n_el:
                  [e.dma_start(out=WSL[:, slot, 0:n_el], in_=WSCR[fam][blk])],
                  ("w", slot), reads=[("wscr", fam, blk)], writes=[("w", slot), ("wl", m)])

        def _req(entry):
            n = wstate["n"]
            wstate["n"] += 1
            if P.dry:
                plan.append(entry)
            else:
                assert plan[n] == entry, (n, plan[n], entry)
                while wstate["issued"] < min(n + NSLOT - 1, len(plan)):
                    w_issue(wstate["issued"])
                    wstate["issued"] += 1
                if wstate["issued"] <= n:
                    w_issue(n)
                    wstate["issued"] = n + 1
            return n % NSLOT

        def get_w(fam, blk):
            slot = _req((fam, blk))
            kk = MIXC if fam == "wbr" else KC
            return WSL[:, slot, 0:kk * 128].rearrange("p (k c) -> p k c", k=kk), ("w", slot)

        def get_w_pair(fam, blk):
            if wstate["n"] % 2 == 1:
                _req(None)
            s0 = _req((fam, blk))
            s1 = _req((fam, blk + 1))
            assert s1 == s0 + 1
            return (WSL[:, s0:s0 + 2, 0:KC * 128].rearrange("p s (k c) -> p s k c", k=KC),
                    [("w", s0), ("w", s1)])

        def load_consts():
            P.dma("pool", lambda e: [e.dma_start(out=cst[:], in_=CONST)], "cst", writes=["cst"])
            P.dma("sp", lambda e: [e.dma_start(out=sinkt[:], in_=SINK.partition_broadcast(128))], "sinkt",
                  writes=["sinkt"])
            P.op("act", lambda e: e.activation(out=esink[:], in_=sinkt[:], func=AF.Exp), reads=["sinkt"],
                 writes=["esink"])

        def load_g(which):
            src = {"gn": GN, "gm": GM, "gf": GF}[which]
            P.dma("sp", lambda e: [e.dma_start(out=gt[:], in_=src.partition_broadcast(128))], "gt", writes=["gt"])

        def front(xsrc, runs, b, dst, dstkeyf, col0, bankf):
            xb = xo[b]
            par = fcnt["n"] % 2
            fcnt["n"] += 1
            hb = hbs[par]
            hk = ("hb", par)
            sk = ("stat", par)
            so = 4 * par

            def ld(e):
                r = []
                for (p0, cn, row0, rs) in runs:
                    if rs == 1:
                        src = xsrc[row0:row0 + cn, :]
                    else:
                        src = xsrc[row0:row0 + cn * rs, :].rearrange("(i s) d -> i s d", s=rs)[:, 0, :]
                    r.append(e.dma_start(out=xb[p0:p0 + cn, :], in_=src))
                return r

            P.dma("sp", ld, ("xo", b), writes=[("xo", b)], n=len(runs))
            P.op("act", lambda e: e.activation(out=hb[:], in_=xb[:], func=AF.Square, accum_out=stat[:, so:so + 1]),
                 reads=[("xo", b)], writes=[hk, sk])
            P.op("dve", lambda e: e.tensor_scalar(out=stat[:, so + 1:so + 2], in0=stat[:, so:so + 1], scalar1=1.0 / D,
                                                  scalar2=1e-6, op0=ALU.mult, op1=ALU.add), reads=[sk], writes=[sk])
            P.op("act", lambda e: e.activation(out=stat[:, so + 2:so + 3], in_=stat[:, so + 1:so + 2], func=AF.Sqrt),
                 reads=[sk], writes=[sk])
            P.op("dve", lambda e: e.reciprocal(out=stat[:, so + 3:so + 4], in_=stat[:, so + 2:so + 3]), reads=[sk],
                 writes=[sk])
            P.op("dve", lambda e: e.scalar_tensor_tensor(out=hb[:], in0=xb[:], scalar=stat[:, so + 3:so + 4], in1=gt[:],
                                                         op0=ALU.mult, op1=ALU.mult),
                 reads=[("xo", b), sk, "gt"], writes=[hk])
            for q in range(KC // G4):
                bk = bankf()
                pt = PS[bk][:, :].bitcast(BF16)
                for i in range(G4):
                    kc = q * G4 + i
                    P.op("pe", lambda e, pt=pt, i=i, kc=kc: e.transpose(out=pt[:, i * 128:(i + 1) * 128],
                                                                       in_=hb[:, kc * 128:(kc + 1) * 128],
                                                                       identity=ident),
                         reads=[hk, "cst"], writes=[("ps", bk)])
                src = pt[:, 0:G4 * 128].rearrange("p (a b) -> p a b", a=G4)
                dv = dst[:, q * G4:(q + 1) * G4, col0:col0 + 128]
                if q % 2 == 0:
                    P.op("act", lambda e, src=src, dv=dv: e.activation(out=dv, in_=src, func=AF.Copy),
                         reads=[("ps", bk)], writes=dstkeyf(q))
                else:
                    P.op("dve", lambda e, src=src, dv=dv: e.tensor_copy(out=dv, in_=src),
                         reads=[("ps", bk)], writes=dstkeyf(q))

        R = lambda j: ("R", j)
        hq_keys = lambda q: [R(j) for j in range(q * G4, (q + 1) * G4)]
        kvq_keys = lambda q: [R(j) for j in range(2 * q * G4, 2 * (q + 1) * G4)]
        h_kc = lambda kc: [R(kc)]
        kv_kc = lambda kc: [R(2 * kc), R(2 * kc + 1)]

        def gemm_fm(w, wkey, src3, srckeys, ntok, bk):
            pt = PS[bk]
            for kc in range(KC):
                P.op("pe", lambda e, kc=kc: e.matmul(pt[:, 0:ntok], w[:, kc, :], src3[:, kc, 0:ntok],
                                                     start=(kc == 0), stop=(kc == KC - 1)),
                     reads=[wkey] + srckeys(kc), writes=[("ps", bk)])
            return pt

        def gemm_q(name):
            w, wk = get_w("win", bidx[name])
            bk = gbank()
            gemm_fm(w, wk, hT, h_kc, TQ, bk)
            return bk

        def rope_a(bk, n):
            qi, qb_ = rot("q", qbf)
            P.op("act", lambda e: e.activation(out=qb_[:, 0:n], in_=PS[bk][:, 0:n], func=AF.Copy),
                 reads=[("ps", bk)], writes=[("qbf", qi)])
            return (bk, n, qi, qb_)

        def rope_b(hnd, dest, destkeys):
            bk, n, qi, qb_ = hnd
            pt = PS[bk]
            ti, t1 = 0, t1b[0]
            t2 = t2b[0]
            rp = PS[BK_ROT]
            P.op("pe", lambda e: e.matmul(rp[:, 0:n], rotm, qb_[:, 0:n], start=True, stop=True),
                 reads=[("qbf", qi), "cst"], writes=[("ps", BK_ROT)])
            P.op("dve", lambda e: e.tensor_tensor(out=t1[:, 0:n], in0=pt[:, 0:n], in1=cosb[:, 0:n], op=ALU.mult),
                 reads=[("ps", bk), "cos"], writes=[("t1", ti)])
            P.op("dve", lambda e: e.tensor_tensor(out=t2[:, 0:n], in0=rp[:, 0:n], in1=sinb[:, 0:n], op=ALU.mult),
                 reads=[("ps", BK_ROT), "sin"], writes=["t2"])
            P.op("dve", lambda e: e.tensor_tensor(out=dest, in0=t1[:, 0:n], in1=t2[:, 0:n], op=ALU.add),
                 reads=[("t1", ti), "t2"], writes=destkeys)

        def phase_m(seg):
            load_g("gm")
            for sub in range(2):
                front(MEM[seg], [(0, 128, sub * 128, 1)], sub, hT, hq_keys, sub * 128, gbank)
            for h in range(4):
                for c in range(2):
                    w, wk = get_w("wmem", h * 2 + c)
                    bk = gbank()
                    pt = gemm_fm(w, wk, hT, h_kc, 256, bk)
                    P.op("act", lambda e, pt=pt, h=h, c=c: e.activation(out=KmT[:, h * 2 + c, :], in_=pt[:, 0:256],
                                                                        func=AF.Copy),
                         reads=[("ps", bk)], writes=["KmT"])
            for h in range(4):
                w4, wks = get_w_pair("wmem", 8 + 2 * h)
                for sub in range(2):
                    bk = gbank()
                    pv = PS[bk][:, 0:256].rearrange("p (s c) -> p s c", s=2)
                    for kc in range(KC):
                        P.op("pe", lambda e, pv=pv, kc=kc, sub=sub, w4=w4: e.matmul(
                            pv, hT[:, kc, sub * 128:(sub + 1) * 128], w4[:, :, kc, :],
                            start=(kc == 0), stop=(kc == KC - 1)), reads=wks + h_kc(kc), writes=[("ps", bk)])
                    P.op("dve", lambda e, bk=bk, h=h, sub=sub: e.tensor_copy(out=Vm[:, sub, h * 256:(h + 1) * 256],
                                                                             in_=PS[bk][:, 0:256]),
                         reads=[("ps", bk)], writes=["Vm"])

        cw_keys = {}

        def phase_kv(seg, grp):
            d, H, L, npc, tiles = kv_tiles(seg, grp)
            base = MAXH - H
            gi = {"n": 0, "d4": 1, "d16": 2}[grp]
            kblocks = [(("ak", gi, h), KTC[seg, "a", gi], h) for h in range(8)]
            vblocks = [(("av", gi, 2 * hp), VC[seg, "a", gi], hp * 256) for hp in range(4)]
            if grp == "n":
                kblocks += [(("bk", h), KTC[seg, "b", 0], h) for h in range(4)]
                vblocks += [(("bv", 2 * hp), VC[seg, "b", 0], hp * 256) for hp in range(2)]
            keys = cw_keys.setdefault((seg, grp), [])
            load_g("gn")
            for (i0, nt) in tiles:
                nsub = nt // 128
                for sub in range(nsub):
                    runs = []
                    idx = i0 + sub * 128
                    p0 = 0
                    while p0 < 128:
                        r, j = divmod(idx + p0, npc)
                        cn = min(128 - p0, npc - j)
                        runs.append((p0, cn, base + r + d * j, d))
                        p0 += cn
                    front(X[seg], runs, sub % 2, hTkv, kvq_keys, sub * 128, gbank4)
                P.dma("sp", lambda e, i0=i0, nt=nt: [e.dma_start(out=cosb[:, 0:nt], in_=COSK[seg, grp][:, i0:i0 + nt])],
                      "cos", writes=["cos"])
                P.dma("sp", lambda e, i0=i0, nt=nt: [e.dma_start(out=sinb[:, 0:nt], in_=SINK_[seg, grp][:, i0:i0 + nt])],
                      "sin", writes=["sin"])

                def k_finish(pend):
                    hnd, cache, h = pend
                    ki, ko = rot("k", kout)
                    rope_b(hnd, ko[:, 0:nt], [("kout", ki)])
                    key = ("cw", seg, grp, len(keys))
                    keys.append(key)
                    P.dma("sp", lambda e, ko=ko, cache=cache, h=h, i0=i0, nt=nt:
                          [e.dma_start(out=cache[h][:, i0:i0 + nt], in_=ko[:, 0:nt])],
                          ("kout", ki), reads=[("kout", ki)], writes=[key])

                pend = None
                for (bn, cache, h) in kblocks:
                    w, wk = get_w("win", bidx[bn])
                    bk = gbank4()
                    gemm_fm(w, wk, hTkv, kv_kc, nt, bk)
                    hnd = rope_a(bk, nt)
                    if pend is not None:
                        k_finish(pend)
                    pend = (hnd, cache, h)
                k_finish(pend)
                for (bn, cache, c0) in vblocks:
                    w4, wks = get_w_pair("win", bidx[bn])
                    for sub in range(nsub):
                        bk = gbank4()
                        pv = PS[bk][:, 0:256].rearrange("p (s c) -> p s c", s=2)
                        for kc in range(KC):
                            P.op("pe", lambda e, pv=pv, kc=kc, sub=sub, w4=w4: e.matmul(
                                pv, hTkv[:, kc, sub * 128:(sub + 1) * 128], w4[:, :, kc, :],
                                start=(kc == 0), stop=(kc == KC - 1)), reads=wks + kv_kc(kc), writes=[("ps", bk)])
                        vi, vo = rot("v", vout)
                        if sub % 2 == 0:
                            P.op("act", lambda e, bk=bk, vo=vo: e.activation(out=vo[:], in_=PS[bk][:, 0:256], func=AF.Copy),
                                 reads=[("ps", bk)], writes=[("vout", vi)])
                        else:
                            P.op("dve", lambda e, bk=bk, vo=vo: e.tensor_copy(out=vo[:], in_=PS[bk][:, 0:256]),
                                 reads=[("ps", bk)], writes=[("vout", vi)])
                        key = ("cw", seg, grp, len(keys))
                        keys.append(key)
                        r0 = i0 + sub * 128
                        P.dma("sp", lambda e, vo=vo, cache=cache, r0=r0, c0=c0:
                              [e.dma_start(out=cache[r0:r0 + 128, c0:c0 + 256], in_=vo[:])],
                              ("vout", vi), reads=[("vout", vi)], writes=[key])
            P.op("sp", lambda e: e.nop(), reads=list(keys), writes=[("cache", seg, grp)])

        def q_rope_finish(hnd):
            qi, qt = rot("qT", qT)
            rope_b(hnd, qt[:, :], [("qT", qi)])
            return qi, qt

        def evac_q(bk):
            qi, qt = rot("qT", qT)
            P.op("act", lambda e: e.activation(out=qt[:, :], in_=PS[bk][:, 0:TQ], func=AF.Copy),
                 reads=[("ps", bk)], writes=[("qT", qi)])
            return qi, qt

        def silu_z(bk):
            zi, z = rot("z", zs)
            P.op("act", lambda e: e.activation(out=z[:], in_=PS[bk][:, 0:TQ], func=AF.Silu),
                 reads=[("ps", bk)], writes=[("zs", zi)])
            return zi, z

        def score_exp(mms, rows_lo, rows_hi, scale, mask_lo, mask_hi, extra_reads):
            bk = sbank()
            pt = PS[bk]
            for (half, c0, ncol, nrow, lhs, rhs) in mms:
                for i, (l, r) in enumerate(zip(lhs, rhs)):
                    P.op("pe", lambda e, half=half, c0=c0, ncol=ncol, nrow=nrow, l=l, r=r, i=i, nl=len(lhs):
                         e.matmul(pt[0:nrow, half * 256 + c0:half * 256 + c0 + ncol], l, r, start=(i == 0),
                                  stop=(i == nl - 1)),
                         reads=extra_reads, writes=[("ps", bk)])
            ei, E = rot("E", Eb)
            pi, PT = rot("PT", PTb)
            for half, rows, mask in ((0, rows_lo, mask_lo), (1, rows_hi, mask_hi)):
                if rows == 0:
                    continue
                sl = slice(half * 256, half * 256 + 256)
                if mask is None:
                    P.op("act", lambda e, rows=rows, sl=sl: e.activation(out=PT[0:rows, sl], in_=pt[0:rows, sl],
                                                                         func=AF.Exp, scale=scale),
                         reads=[("ps", bk)], writes=[("PT", pi)])
                else:
                    P.op("act", lambda e, rows=rows, sl=sl: e.activation(out=E[0:rows, sl], in_=pt[0:rows, sl],
                                                                         func=AF.Exp, scale=scale),
                         reads=[("ps", bk)], writes=[("E", ei)])
                    P.op("dve", lambda e, rows=rows, sl=sl, mask=mask: e.tensor_tensor(
                        out=PT[0:rows, sl], in0=E[0:rows, sl], in1=mask[0:rows, :], op=ALU.mult),
                        reads=[("E", ei), "cst"], writes=[("PT", pi)])
            return pi, PT

        def finish_head(parts, extra_den, z, zi, dest, destkey):
            bks = [("ps", b) for b in BK_ACC]

            def vw(ps_ap, sb_tile, U):
                if U is None:
                    return ps_ap, sb_tile[:]
                return (ps_ap.rearrange("p (u j) -> p j u", u=U), sb_tile[:].rearrange("p (j u) -> p j u", u=U))

            acc0, den0, _ = parts[0]
            if extra_den is not None:
                P.op("dve", lambda e: e.tensor_scalar_add(out=tsum[:], in0=den0, scalar1=extra_den),
                     reads=bks + ["esink"], writes=["tsum"])
            else:
                P.op("dve", lambda e: e.tensor_copy(out=tsum[:], in_=den0), reads=bks, writes=["tsum"])
            for (_, dn, U) in parts[1:]:
                pv, sv = vw(dn, tsum, U)
                P.op("dve", lambda e, pv=pv, sv=sv: e.tensor_tensor(out=sv, in0=pv, in1=sv, op=ALU.add),
                     reads=bks + ["tsum"], writes=["tsum"])
            P.op("dve", lambda e: e.reciprocal(out=rden[:], in_=tsum[:]), reads=["tsum"], writes=["rden"])
            if len(parts) == 1:
                P.op("dve", lambda e: e.tensor_tensor(out=tnum[:], in0=acc0, in1=rden[:], op=ALU.mult),
                     reads=bks + ["rden"], writes=["tnum"])
            else:
                P.op("dve", lambda e: e.tensor_copy(out=tnum[:], in_=acc0), reads=bks, writes=["tnum"])
                for (ac, _, U) in parts[1:]:
                    pv, sv = vw(ac, tnum, U)
                    P.op("dve", lambda e, pv=pv, sv=sv: e.tensor_tensor(out=sv, in0=pv, in1=sv, op=ALU.add),
                         reads=bks + ["tnum"], writes=["tnum"])
                P.op("dve", lambda e: e.tensor_tensor(out=tnum[:], in0=tnum[:], in1=rden[:], op=ALU.mult),
                     reads=["tnum", "rden"], writes=["tnum"])
            P.op("dve", lambda e: e.tensor_tensor(out=dest, in0=tnum[:], in1=z[:], op=ALU.mult),
                 reads=["tnum", ("zs", zi)], writes=[destkey])

        def phase_q(seg):
            core = SEG_CORE[seg]
            nt_ = core // TQ
            sc = float(1.0 / np.sqrt(128.0))
            scm = 1.0 / 16.0
            c1, c2, c3 = KTC[seg, "a", 0], KTC[seg, "a", 1], KTC[seg, "a", 2]
            V1, V2, V3 = VC[seg, "a", 0], VC[seg, "a", 1], VC[seg, "a", 2]
            ck = [("cache", seg, g) for g in ("n", "d4", "d16")]
            A0, A1, A2 = (PS[b] for b in BK_ACC)
            acc = [A0[:, 0:256], A0[:, 256:512], A1[:, 0:256]]
            den = [A1[:, 256:512], A2[:, 0:256], A2[:, 256:512]]
            acck = [("ps", BK_ACC[0]), ("ps", BK_ACC[0]), ("ps", BK_ACC[1])]
            denk = [("ps", BK_ACC[1]), ("ps", BK_ACC[2]), ("ps", BK_ACC[2])]

            def val(col, rows):
                return valw[0:rows, col:col + 1].to_broadcast([rows, 128])

            for t in range(min(nt_, DEBUG_QT)):
                tok0 = t * TQ
                load_g("gn")
                for sub in range(2):
                    front(X[seg], [(0, 128, MAXH + tok0 + sub * 128, 1)], sub, hT, hq_keys, sub * 128, gbank)
                P.dma("sp", lambda e, tok0=tok0: [e.dma_start(out=cosb[:, 0:TQ], in_=COSQ[seg][:, tok0:tok0 + TQ])],
                      "cos", writes=["cos"])
                P.dma("sp", lambda e, tok0=tok0: [e.dma_start(out=sinb[:, 0:TQ], in_=SINQ[seg][:, tok0:tok0 + TQ])],
                      "sin", writes=["sin"])
                P.dma("pool", lambda e, t=t: [e.dma_start(out=valw[:], in_=VALW[seg][t])], "valw", writes=["valw"])

                def a_proj1(s):
                    hs = []
                    qs = []
                    for g in range(2):
                        bk = gemm_q(("qa", s, g))
                        hs.append(rope_a(bk, TQ))
                        if g >= 1:
                            qs.append(q_rope_finish(hs[g - 1]))
                    return hs, qs

                def a_proj2(s, part):
                    hs, qs = part
                    bk = gemm_q(("qa", s, 2))
                    hs.append(rope_a(bk, TQ))
                    qs.append(q_rope_finish(hs[1]))
                    bk = gemm_q(("za", s))
                    qs.append(q_rope_finish(hs[2]))
                    z = silu_z(bk)
                    return qs, z

                def a_win(s):
                    cs_ = slice(s * 128, (s + 1) * 128)
                    P.dma("sp", lambda e, s=s, tok0=tok0, cs_=cs_: [
                        e.dma_start(out=k1[:], in_=c1[s][:, tok0 + 64:tok0 + 448]),
                        e.dma_start(out=v1[:], in_=V1[tok0 + 64:tok0 + 448, cs_].rearrange("(c p) d -> p c d", p=128)),
                    ], "win1", reads=[ck[0]], writes=["k1", "v1"], n=2)
                    j4 = tok0 // 4
                    P.dma("sp", lambda e, s=s, j4=j4, cs_=cs_: [
                        e.dma_start(out=k2[:], in_=c2[s].rearrange("p (u n) -> p u n", u=4)[:, :, j4:j4 + 192]),
                        e.dma_start(out=v2lo[:], in_=V2.rearrange("(u n) d -> n u d", u=4)[j4:j4 + 128, :, cs_]),
                        e.dma_start(out=v2hi[0:64], in_=V2.rearrange("(u n) d -> n u d", u=4)[j4 + 128:j4 + 192, :, cs_]),
                    ], "win2", reads=[ck[1]], writes=["k2", "v2"], n=3)
                    j16 = tok0 // 16
                    P.dma("sp", lambda e, s=s, j16=j16, cs_=cs_: [
                        e.dma_start(out=k3[:], in_=c3[s].rearrange("p (u n) -> p u n", u=16)[:, :, j16:j16 + 144]),
                        e.dma_start(out=v3lo[:], in_=V3.rearrange("(u n) d -> n u d", u=16)[j16:j16 + 128, :, cs_]),
                        e.dma_start(out=v3hi[0:16], in_=V3.rearrange("(u n) d -> n u d", u=16)[j16 + 128:j16 + 144, :, cs_]),
                    ], "win3", reads=[ck[2]], writes=["k3", "v3"], n=3)

                def a_geom(g, qt):
                    if g == 0:
                        return dict(U=2, nq=128, klo=lambda u: k1[:, u * 128:(u + 1) * 128],
                                    khi=lambda u: k1[:, (u + 1) * 128:(u + 2) * 128],
                                    vlo=lambda u: v1[:, u, :], vhi=lambda u: v1[:, u + 1, :],
                                    vallo=lambda u: val(0 + u, 128), valhi=lambda u: val(1 + u, 128),
                                    qcol=(lambda u: qt[:, u * 128:(u + 1) * 128]) if qt is not None else None,
                                    rk=["k1", "v1"])
                    if g == 1:
                        return dict(U=4, nq=64, klo=lambda u: k2[:, u, 0:128], khi=lambda u: k2[:, u, 128:192],
                                    vlo=lambda u: v2lo[:, u, :], vhi=lambda u: v2hi[0:64, u, :],
                                    vallo=lambda u: val(3 + u, 128), valhi=lambda u: val(7 + u, 64),
                                    qcol=(lambda u: qt[:, :].rearrange("p (j u) -> p u j", u=4)[:, u, :]) if qt is not None else None,
                                    rk=["k2", "v2"])
                    return dict(U=16, nq=16, klo=lambda u: k3[:, u, 0:128], khi=lambda u: k3[:, u, 128:144],
                                vlo=lambda u: v3lo[:, u, :], vhi=lambda u: v3hi[0:16, u, :],
                                vallo=lambda u: val(11 + u, 128), valhi=lambda u: val(27 + u, 16),
                                qcol=(lambda u: qt[:, :].rearrange("p (j u) -> p u j", u=16)[:, u, :]) if qt is not None else None,
                                rk=["k3", "v3"])

                def a_scores(qs):
                    pts = []
                    for g in range(3):
                        qi, qt = qs[g]
                        G = a_geom(g, qt)
                        U, nq = G["U"], G["nq"]
                        mms = []
                        for u in range(U):
                            mms.append((0, u * nq, nq, 128, [G["klo"](u)], [G["qcol"](u)]))
                            mms.append((1, u * nq, nq, nq, [G["khi"](u)], [G["qcol"](u)]))
                        pts.append(score_exp(mms, 128, nq, sc, masks[:, 2 * g, :], masks[:, 2 * g + 1, :],
                                             [("qT", qi)] + G["rk"]))
                    return pts

                def a_pv(pts):
                    for g in range(3):
                        pi, PT = pts[g]
                        G = a_geom(g, None)
                        U, nq, rk = G["U"], G["nq"], G["rk"]
                        for u in range(U):
                            cs2 = slice(u * nq, (u + 1) * nq)
                            cs2h = slice(256 + u * nq, 256 + (u + 1) * nq)
                            vlo, vhi, vallo, valhi = G["vlo"](u), G["vhi"](u), G["vallo"](u), G["valhi"](u)
                            P.op("pe", lambda e, g=g, cs2=cs2, vlo=vlo, PT=PT: e.matmul(
                                acc[g][:, cs2], vlo, PT[:, cs2], start=True, stop=False),
                                reads=[("PT", pi)] + rk, writes=[acck[g]])
                            P.op("pe", lambda e, g=g, cs2=cs2, cs2h=cs2h, vhi=vhi, PT=PT, nq=nq: e.matmul(
                                acc[g][:, cs2], vhi, PT[0:nq, cs2h], start=False, stop=True),
                                reads=[("PT", pi)] + rk, writes=[acck[g]])
                            P.op("pe", lambda e, g=g, cs2=cs2, vallo=vallo, PT=PT: e.matmul(
                                den[g][:, cs2], vallo, PT[:, cs2], start=True, stop=False),
                                reads=[("PT", pi), "valw"], writes=[denk[g]])
                            P.op("pe", lambda e, g=g, cs2=cs2, cs2h=cs2h, valhi=valhi, PT=PT, nq=nq: e.matmul(
                                den[g][:, cs2], valhi, PT[0:nq, cs2h], start=False, stop=True),
                                reads=[("PT", pi), "valw"], writes=[denk[g]])

                st_ = a_proj2(0, a_proj1(0))
                a_win(0)
                for s in range(8):
                    qs, (zi, z) = st_
                    if s < 7:
                        part = a_proj1(s + 1)
                    pts = a_scores(qs)
                    if s < 7:
                        nxt = a_proj2(s + 1, part)
                    a_pv(pts)
                    if s < 7:
                        a_win(s + 1)
                    parts = [(acc[0], den[0], None), (acc[1], den[1], 4), (acc[2], den[2], 16)]
                    finish_head(parts, None, z, zi, bzT[:, s, :], R(KC + s))
                    if s < 7:
                        st_ = nxt

                def b_win(j):
                    P.dma("sp", lambda e, j=j, tok0=tok0: [
                        e.dma_start(out=kbw[:], in_=KTC[seg, "b", 0][j][:, tok0:tok0 + 512]),
                        e.dma_start(out=vbw[:], in_=VC[seg, "b", 0][tok0:tok0 + 512, j * 128:(j + 1) * 128]
                                    .rearrange("(c p) d -> p c d", p=128)),
                    ], "winb", reads=[("cache", seg, "n")], writes=["kb", "vb"], n=2)

                def b_q(h):
                    bk = gemm_q(("qb", h))
                    return rope_a(bk, TQ)

                qcur = []
                hnd = None
                for hq in range(4):
                    h2 = b_q(hq)
                    if hnd is not None:
                        qcur.append(q_rope_finish(hnd))
                    hnd = h2
                qcur.append(q_rope_finish(hnd))
                b_win(0)
                A0b, A1b = PS[BK_ACC[0]], PS[BK_ACC[1]]
                for j in range(4):
                    qnext = []
                    for hq in range(4):
                        h = 4 * j + hq
                        qi, qt = qcur[hq]
                        mm_pn, mm_c = [], []
                        for sb in range(2):
                            qc = qt[:, sb * 128:(sb + 1) * 128]
                            mm_pn.append((0, sb * 128, 128, 128, [kbw[:, sb * 128:(sb + 1) * 128]], [qc]))
                            mm_pn.append((1, sb * 128, 128, 128, [kbw[:, (sb + 2) * 128:(sb + 3) * 128]], [qc]))
                            mm_c.append((0, sb * 128, 128, 128, [kbw[:, (sb + 1) * 128:(sb + 2) * 128]], [qc]))
                        p1, PT1 = score_exp(mm_pn, 128, 128, sc, masks[:, 0, :], masks[:, 1, :], [("qT", qi), "kb"])
                        p2, PT2 = score_exp(mm_c, 128, 0, sc, None, None, [("qT", qi), "kb"])
                        bkz = gemm_q(("zb", h))
                        zi, z = silu_z(bkz)
                        if j < 3:
                            hn = b_q(4 * (j + 1) + hq)
                        for sb in range(2):
                            cs2 = slice(sb * 128, (sb + 1) * 128)
                            cs2h = slice(256 + sb * 128, 256 + (sb + 1) * 128)
                            seq = [(vbw[:, sb, :], val(43 + sb, 128), PT1[:, cs2], ("PT", p1)),
                                   (vbw[:, sb + 1, :], val(44 + sb, 128), PT2[:, cs2], ("PT", p2)),
                                   (vbw[:, sb + 2, :], val(45 + sb, 128), PT1[:, cs2h], ("PT", p1))]
                            for n_, (vv, vl, pp, pk) in enumerate(seq):
                                P.op("pe", lambda e, vv=vv, pp=pp, cs2=cs2, n_=n_: e.matmul(
                                    A0b[:, cs2], vv, pp, start=(n_ == 0), stop=(n_ == 2)),
                                    reads=[pk, "vb"], writes=[("ps", BK_ACC[0])])
                                P.op("pe", lambda e, vl=vl, pp=pp, cs2=cs2, n_=n_: e.matmul(
                                    A1b[:, cs2], vl, pp, start=(n_ == 0), stop=(n_ == 2)),
                                    reads=[pk, "valw"], writes=[("ps", BK_ACC[1])])
                        if j < 3:
                            qnext.append(q_rope_finish(hn))
                        finish_head([(A0b[:, 0:256], A1b[:, 0:256], None)], esink[:, h:h + 1], z, zi,
                                    bzT[:, 8 + h, :], R(KC + 8 + h))
                    if j < 3:
                        b_win(j + 1)
                        qcur = qnext

                for h in range(4):
                    mq = []
                    for c in range(2):
                        bk = gemm_q(("mq", h, c))
                        mq.append(evac_q(bk))
                    mms = []
                    for kc2 in range(2):
                        mms.append((kc2, 0, 256, 128,
                                    [KmT[:, h * 2 + c, kc2 * 128:(kc2 + 1) * 128] for c in range(2)],
                                    [mq[c][1][:, :] for c in range(2)]))
                    pi, PT = score_exp(mms, 128, 128, scm, None, None, [("qT", mq[0][0]), ("qT", mq[1][0]), "KmT"])
                    zz = []
                    for c in range(2):
                        bk = gemm_q(("zm", h, c))
                        zz.append(silu_z(bk))
                    for kc2 in range(2):
                        P.op("pe", lambda e, kc2=kc2, PT=PT: e.matmul(A1b[:, 0:256], ones, PT[:, kc2 * 256:(kc2 + 1) * 256],
                                                                        start=(kc2 == 0), stop=(kc2 == 1)),
                             reads=[("PT", pi), "cst"], writes=[("ps", BK_ACC[1])])
                    for c in range(2):
                        for kc2 in range(2):
                            P.op("pe", lambda e, kc2=kc2, c=c, PT=PT, h=h: e.matmul(
                                A0b[:, c * 256:(c + 1) * 256], Vm[:, kc2, h * 256 + c * 128:h * 256 + (c + 1) * 128],
                                PT[:, kc2 * 256:(kc2 + 1) * 256], start=(kc2 == 0), stop=(kc2 == 1)),
                                reads=[("PT", pi), "Vm"], writes=[("ps", BK_ACC[0])])
                    for c in range(2):
                        zi, z = zz[c]
                        finish_head([(A0b[:, c * 256:(c + 1) * 256], A1b[:, 0:256], None)], None, z, zi,
                                    bzT[:, 24 + 2 * h + c, :], R(KC + 24 + 2 * h + c))

                bzk = [R(KC + i) for i in range(MIXC)]
                br_rows = [(0, 8), (8, 24), (24, 32)]
                for dc in range(KC):
                    for bi, nm in enumerate(("ga", "gb", "gm")):
                        bk = gemm_q((nm, dc))
                        P.op("act", lambda e, bk=bk, bi=bi: e.activation(out=gs[bi][:], in_=PS[bk][:, 0:TQ],
                                                                         func=AF.Sigmoid),
                             reads=[("ps", bk)], writes=[("gs", bi)])
                    wb, wbk = get_w("wbr", dc)
                    for bi in range(3):
                        gsb = gs[bi]
                        bk2 = gbank()
                        r0, r1 = br_rows[bi]
                        for rc in range(r0, r1):
                            P.op("pe", lambda e, bk2=bk2, rc=rc, wb=wb, r0=r0, r1=r1: e.matmul(
                                PS[bk2][:, 0:TQ], wb[:, rc, :], bzT[:, rc, :],
                                start=(rc == r0), stop=(rc == r1 - 1)),
                                reads=[wbk, bzk[rc]], writes=[("ps", bk2)])
                        if bi == 0:
                            P.op("dve", lambda e, bk2=bk2, gsb=gsb: e.tensor_tensor(
                                out=uacc[:], in0=PS[bk2][:, 0:TQ], in1=gsb[:], op=ALU.mult),
                                reads=[("ps", bk2), ("gs", bi)], writes=["uacc"])
                        else:
                            P.op("dve", lambda e, bk2=bk2, gsb=gsb: e.tensor_tensor(
                                out=utmp[:], in0=PS[bk2][:, 0:TQ], in1=gsb[:], op=ALU.mult),
                                reads=[("ps", bk2), ("gs", bi)], writes=["utmp"])
                            if bi == 1:
                                P.op("dve", lambda e: e.tensor_tensor(out=uacc[:], in0=uacc[:], in1=utmp[:], op=ALU.add),
                                     reads=["uacc", "utmp"], writes=["uacc"])
                            else:
                                P.op("dve", lambda e, dc=dc: e.tensor_tensor(
                                    out=uT[:, dc, :], in0=uacc[:], in1=utmp[:], op=ALU.add),
                                    reads=["uacc", "utmp"], writes=[R(KC + MIXC + dc)])

                uk = [R(KC + MIXC + i) for i in range(KC)]
                for sub in range(2):
                    P.dma("sp", lambda e, sub=sub, tok0=tok0: [
                        e.dma_start(out=xo[sub][:], in_=X[seg][MAXH + tok0 + sub * 128:MAXH + tok0 + (sub + 1) * 128, :])],
                        ("xo", sub), writes=[("xo", sub)])
                load_g("gf")
                for ob in range(KC // 2):
                    w4, wks = get_w_pair("wout", 2 * ob)
                    for sub in range(2):
                        bk = gbank()
                        pv = PS[bk][:, 0:256].rearrange("p (s c) -> p s c", s=2)
                        for kc in range(KC):
                            P.op("pe", lambda e, pv=pv, kc=kc, sub=sub, w4=w4: e.matmul(
                                pv, uT[:, kc, sub * 128:(sub + 1) * 128], w4[:, :, kc, :],
                                start=(kc == 0), stop=(kc == KC - 1)), reads=wks + [uk[kc]], writes=[("ps", bk)])
                        P.op("dve", lambda e, bk=bk, sub=sub, ob=ob: e.tensor_tensor(
                            out=xo[sub][:, ob * 256:(ob + 1) * 256], in0=PS[bk][:, 0:256],
                            in1=xo[sub][:, ob * 256:(ob + 1) * 256], op=ALU.add),
                            reads=[("ps", bk), ("xo", sub)], writes=[("xo", sub)])
                for sub in range(2):
                    xb = xo[sub]
                    c0 = 4 * sub
                    hbj = hbs[sub]
                    sk2 = ("stat2", sub)
                    P.op("act", lambda e, xb=xb, c0=c0, hbj=hbj: e.activation(out=hbj[:], in_=xb[:], func=AF.Square,
                                                                              accum_out=stat2[:, c0:c0 + 1]),
                         reads=[("xo", sub)], writes=[("hb", sub), sk2])
                    P.op("dve", lambda e, c0=c0: e.tensor_scalar(out=stat2[:, c0 + 1:c0 + 2], in0=stat2[:, c0:c0 + 1],
                                                                 scalar1=1.0 / D, scalar2=1e-6, op0=ALU.mult, op1=ALU.add),
                         reads=[sk2], writes=[sk2])
                    P.op("act", lambda e, c0=c0: e.activation(out=stat2[:, c0 + 2:c0 + 3], in_=stat2[:, c0 + 1:c0 + 2],
                                                              func=AF.Sqrt), reads=[sk2], writes=[sk2])
                    P.op("dve", lambda e, c0=c0: e.reciprocal(out=stat2[:, c0 + 3:c0 + 4], in_=stat2[:, c0 + 2:c0 + 3]),
                         reads=[sk2], writes=[sk2])
                    P.op("dve", lambda e, xb=xb, c0=c0: e.scalar_tensor_tensor(
                        out=xb[:], in0=xb[:], scalar=stat2[:, c0 + 3:c0 + 4], in1=gt[:], op0=ALU.mult, op1=ALU.mult),
                        reads=[("xo", sub), sk2, "gt"], writes=[("xo", sub)])
                    P.dma("sp", lambda e, xb=xb, sub=sub, tok0=tok0: [
                        e.dma_start(out=Y[seg][tok0 + sub * 128:tok0 + (sub + 1) * 128, :], in_=xb[:])],
                        ("xo", sub), reads=[("xo", sub)], writes=[("y", seg, t, sub)], final=True)

        def whole():
            wstate["n"] = 0
            for k in cnt:
                cnt[k] = 0
            for k in rr:
                rr[k] = 0
            load_consts()
            stage = 0
            for seg in "PS":
                stage += 1
                if stage > DEBUG_STOP:
                    return
                phase_m(seg)
                for grp in ("n", "d4", "d16"):
                    stage += 1
                    if stage > DEBUG_STOP:
                        return
                    phase_kv(seg, grp)
                stage += 1
                if stage > DEBUG_STOP:
                    return
                phase_q(seg)

        P.dry = True
        whole()
        P.dry = False
        cw_keys.clear()
        whole()
        P.emit()
        info = dict(n_ops=P.n_ops, n_wait=P.n_wait, sbuf_left=nc.sbuf_bytes_remaining, nplan=len(plan))
    return nc, info


def rope_tables(pos):
    inv = (10000.0 ** (-np.arange(0, HD, 2, dtype=np.float32) / HD)).astype(np.float32)
    ang = pos.astype(np.float32)[None, :] * inv[:, None]
    c = np.cos(ang).astype(np.float32)
    s = np.sin(ang).astype(np.float32)
    return np.concatenate([c, c], 0), np.concatenate([s, s], 0)


def const_table():
    ident = np.eye(128, dtype=np.float32)
    rotm = np.zeros((128, 128), np.float32)
    for m in range(64):
        rotm[m + 64, m] = -1.0
        rotm[m, m + 64] = 1.0
    ones = np.ones((128, 128), np.float32)
    kl = np.arange(128)[:, None]
    msk = []
    for nq, U in ((128, 2), (64, 4), (16, 16)):
        ql = np.arange(nq)[None, :]
        lo = (kl >= ql).astype(np.float32)
        hi = ((kl <= ql) & (kl < nq)).astype(np.float32)
        msk.append(np.tile(lo, (1, U)))
        msk.append(np.tile(hi, (1, U)))
    return np.concatenate([ident, rotm, ones] + msk, axis=1).astype(np.float32)


def block_layout(w, cols, KC):
    sub = w[:, cols]
    return np.ascontiguousarray(sub.reshape(KC, 128, len(cols)).transpose(1, 0, 2)).reshape(128, KC * len(cols))


def host_prepare(inputs, D):
    KC = D // 128
    NDB = D // 256
    w_in = np.asarray(inputs["w_in"][0], np.float32)
    w_mem = np.asarray(inputs["w_mem_kv"][0], np.float32)
    w_br = np.asarray(inputs["w_branch"][0], np.float32)
    w_out = np.asarray(inputs["w_out"][0], np.float32)
    blocks = in_col_blocks(D)
    win = np.stack([block_layout(w_in, c, KC) for c in blocks.values()])
    wmem = np.stack([block_layout(w_mem, np.arange(b * 128, (b + 1) * 128), KC) for b in range(16)])
    wbr = np.stack([block_layout(w_br, np.arange(b * 128, (b + 1) * 128), MIXC) for b in range(KC)])
    wout = np.stack([block_layout(w_out, np.arange(b * 128, (b + 1) * 128), KC) for b in range(KC)])
    cst = const_table()
    shared = dict(win=win, wmem=wmem, wbr=wbr, wout=wout, const=cst,
                  gn=np.asarray(inputs["g_norm"], np.float32).reshape(1, D),
                  gm=np.asarray(inputs["g_mem"], np.float32).reshape(1, D),
                  gf=np.asarray(inputs["g_final"], np.float32).reshape(1, D),
                  sink=np.asarray(inputs["attn_sink"], np.float32).reshape(1, 16))
    xs = {"P": np.asarray(inputs["x_prompt"], np.float32), "S": np.asarray(inputs["x_sample"], np.float32)}
    mems = {"P": np.asarray(inputs["mem_prompt"], np.float32), "S": np.asarray(inputs["mem_sample"], np.float32)}
    in_maps = []
    for c in range(NCORES):
        m = dict(shared)
        b = c // 4
        ch = c % 4
        for s in "PS":
            core = SEG_CORE[s]
            Ls = SEG_LEN[s]
            a = ch * core
            xe = np.zeros((core + 2 * MAXH + 16, D), np.float32)
            lo = max(0, a - MAXH)
            hi = min(Ls, a + core + MAXH)
            xe[lo - (a - MAXH):hi - (a - MAXH)] = xs[s][b, lo:hi]
            m["x" + s] = xe
            m["mem" + s] = np.ascontiguousarray(mems[s][b])
            cq, sq = rope_tables(np.arange(a, a + core))
            m["cosq" + s] = cq
            m["sinq" + s] = sq
            for g, (d, H) in GRP.items():
                L = core + 2 * H
                npc = L // d
                idx = np.arange(L)
                r, j = idx // npc, idx % npc
                pos = a - H + r + d * j
                ck, sk = rope_tables(np.clip(pos, 0, Ls - 1))
                m["cosk%s%s" % (s, g)] = ck
                m["sink%s%s" % (s, g)] = sk
            nt = core // TQ
            vw = np.zeros((nt, 128, NVAL), np.float32)
            p = np.arange(128)
            for t in range(nt):
                t0 = a + t * TQ

                def ok(pos):
                    return ((pos >= 0) & (pos < Ls)).astype(np.float32)

                for cc in range(3):
                    vw[t, :, cc] = ok(t0 - 64 + cc * 128 + p)
                for u in range(4):
                    vw[t, :, 3 + u] = ok(t0 + u + 4 * (p - 64))
                    vw[t, :, 7 + u] = ok(t0 + u + 4 * (p + 64))
                for u in range(16):
                    vw[t, :, 11 + u] = ok(t0 + u + 16 * (p - 64))
                    vw[t, :, 27 + u] = ok(t0 + u + 16 * (p + 64))
                for cc in range(4):
                    vw[t, :, 43 + cc] = ok(t0 - 128 + cc * 128 + p)
            m["valw" + s] = vw
        in_maps.append(m)
    return in_maps


_CACHE = {}


def run(inputs, D, trace=False):
    if D not in _CACHE:
        _CACHE[D] = build_program(D)
    nc, info = _CACHE[D]
    in_maps = host_prepare(inputs, D)
    res = run_bass_kernel_spmd(nc, in_maps, core_ids=list(range(NCORES)))
    B = 2
    yp = np.zeros((B, SEG_LEN["P"], D), np.float32)
    ys = np.zeros((B, SEG_LEN["S"], D), np.float32)
    for c in range(NCORES):
        b, ch = c // 4, c % 4
        r = res.results[c]
        yp[b, ch * 1024:(ch + 1) * 1024] = r["yP"]
        ys[b, ch * 2048:(ch + 1) * 2048] = r["yS"]
    return yp, ys


def kernel(x_prompt, x_sample, mem_prompt, mem_sample, g_norm, w_in, attn_sink, g_mem, w_mem_kv, w_branch, w_out,
           g_final):
    D = int(np.asarray(x_prompt).shape[-1])
    inputs = dict(x_prompt=x_prompt, x_sample=x_sample, mem_prompt=mem_prompt, mem_sample=mem_sample, g_norm=g_norm,
                  w_in=w_in, attn_sink=attn_sink, g_mem=g_mem, w_mem_kv=w_mem_kv, w_branch=w_branch, w_out=w_out,
                  g_final=g_final)
    return run(inputs, D)
```

```python
import numpy as np
import concourse.bass as bass
import concourse.mybir as mybir
from concourse.bass_utils import run_bass_kernel_spmd
from contextlib import ExitStack

F32 = mybir.dt.float32
BF16 = mybir.dt.bfloat16
AF = mybir.ActivationFunctionType
ALU = mybir.AluOpType
ENGS = ("pe", "act", "dve", "pool", "sp")


class Op:
    __slots__ = ("eng", "fn", "deps", "is_dma", "sig", "token", "n_dma", "idx")

    def __init__(self, eng, fn, is_dma=False, n_dma=1):
        self.eng = eng
        self.fn = fn
        self.deps = ()
        self.is_dma = is_dma
        self.sig = False
        self.token = None
        self.n_dma = n_dma
        self.idx = -1


class Prog:
    def __init__(self, nc, stack):
        self.nc = nc
        self.stack = stack
        self.ops = {e: [] for e in ENGS}
        self.last_w = {}
        self.readers = {}
        self.dry = False
        self.eng_sem = {e: stack.enter_context(nc.semaphore("s_" + e)) for e in ENGS}
        self.dma_sems = {}
        self.dma_cnt = {}
        self.final_tokens = []
        self.last_dma = {}

    def _track(self, o, reads, writes):
        deps = []
        seen = set()

        def add(d):
            if d is not None and d is not o and id(d) not in seen:
                seen.add(id(d))
                deps.append(d)

        for k in reads:
            add(self.last_w.get(k))
        for k in writes:
            add(self.last_w.get(k))
            for r in self.readers.get(k, ()):
                add(r)
        o.deps = deps
        for k in reads:
            self.readers.setdefault(k, []).append(o)
        for k in writes:
            self.last_w[k] = o
            self.readers[k] = []

    def op(self, eng, fn, reads=(), writes=()):
        if self.dry:
            return None
        ps_r = [k for k in reads if isinstance(k, tuple) and k[0] == "ps"]
        if ps_r:
            reads = [k for k in reads if not (isinstance(k, tuple) and k[0] == "ps")]
            writes = list(writes) + ps_r
        o = Op(eng, fn)
        self._track(o, reads, writes)
        o.idx = len(self.ops[eng])
        self.ops[eng].append(o)
        return o

    def dma(self, eng, fn, semkey, reads=(), writes=(), n=1, final=False):
        if self.dry:
            return None
        o = Op(eng, fn, is_dma=True, n_dma=n)
        if semkey not in self.dma_sems:
            self.dma_sems[semkey] = self.stack.enter_context(self.nc.semaphore("d_%d" % len(self.dma_sems)))
            self.dma_cnt[semkey] = 0
        self.dma_cnt[semkey] += 16 * n
        o.token = (self.dma_sems[semkey], self.dma_cnt[semkey])
        self._track(o, reads, writes)
        prev = self.last_dma.get(semkey)
        if prev is not None and all(prev is not d for d in o.deps):
            o.deps.append(prev)
        self.last_dma[semkey] = o
        o.idx = len(self.ops[eng])
        self.ops[eng].append(o)
        if final:
            self.final_tokens.append(o.token)
        return o

    @staticmethod
    def _need_wait(o, d):
        if d.is_dma:
            return True
        if d.eng == o.eng and o.eng == "pe" and not o.is_dma:
            return False
        return True

    def _reduce(self, o):
        comp = {}
        dmas = {}
        for d in o.deps:
            if not self._need_wait(o, d):
                continue
            if d.is_dma:
                sem, val = d.token
                if sem.num not in dmas or dmas[sem.num][1] < val:
                    dmas[sem.num] = (sem, val)
            else:
                if d.eng not in comp or comp[d.eng].idx < d.idx:
                    comp[d.eng] = d
        return comp, dmas

    def emit(self):
        nc = self.nc
        for e in ENGS:
            for o in self.ops[e]:
                comp, dmas = self._reduce(o)
                o.deps = (comp, dmas)
                for d in comp.values():
                    d.sig = True
        for e in ENGS:
            c = 0
            for o in self.ops[e]:
                if not o.is_dma and o.sig:
                    c += 1
                    o.token = (self.eng_sem[e], c)
        self.n_wait = {e: 0 for e in ENGS}
        self.n_ops = {e: len(self.ops[e]) for e in ENGS}

        def run(ename, eng):
            seen = {}

            def wait(sem, val):
                if seen.get(sem.num, 0) >= val:
                    return
                seen[sem.num] = val
                eng.wait_ge(sem, val)
                self.n_wait[ename] += 1

            for o in self.ops[ename]:
                comp, dmas = o.deps
                for d in comp.values():
                    wait(*d.token)
                for sem, val in dmas.values():
                    wait(sem, val)
                r = o.fn(eng)
                if o.is_dma:
                    if not isinstance(r, (list, tuple)):
                        r = [r]
                    assert len(r) == o.n_dma, (len(r), o.n_dma)
                    for ins in r:
                        ins.then_inc(o.token[0], 16)
                elif o.sig:
                    r.then_inc(o.token[0], 1)
            if ename == "sp":
                for sem, val in self.final_tokens:
                    wait(sem, val)

        with nc.Block() as block:
            @block.tensor
            def _(eng):
                run("pe", eng)

            @block.scalar
            def _(eng):
                run("act", eng)

            @block.vector
            def _(eng):
                run("dve", eng)

            @block.gpsimd
            def _(eng):
                run("pool", eng)

            @block.sync
            def _(eng):
                run("sp", eng)


NCORES = 8
HD = 128
MAXH = 1024
TQ = 256
SEG_CORE = {"P": 1024, "S": 2048}
SEG_LEN = {"P": 4096, "S": 8192}
GRP = {"n": (1, 128), "d4": (4, 256), "d16": (16, 1024)}
MIXC = 32
NVAL = 47
DEBUG_STOP = 100
DEBUG_QT = 100
DEBUG_KV = 100
DEBUG_ROPE = 100
DEBUG_V1 = 0


def in_col_blocks(D):
    o_bq = 9216
    o_bkv = o_bq + 2048
    o_mq = o_bkv + 1024
    o_za = o_mq + 1024
    o_zb = o_za + 1024
    o_zm = o_zb + 2048
    o_ga = o_zm + 1024
    o_gb = o_ga + D
    o_gm = o_gb + D
    ar = np.arange(128)

    def aq(t, g, h):
        return ((t * 3 + g) * 8 + h) * 128 + ar

    blocks = {}
    for g in range(3):
        for h in range(8):
            blocks[("ak", g, h)] = aq(1, g, h)
        for h in range(8):
            blocks[("av", g, h)] = aq(2, g, h)
    for h in range(4):
        blocks[("bk", h)] = o_bkv + h * 128 + ar
    for h in range(4):
        blocks[("bv", h)] = o_bkv + (4 + h) * 128 + ar
    for s in range(8):
        for g in range(3):
            blocks[("qa", s, g)] = aq(0, g, s)
        blocks[("za", s)] = o_za + s * 128 + ar
    for h in range(16):
        blocks[("qb", h)] = o_bq + h * 128 + ar
        blocks[("zb", h)] = o_zb + h * 128 + ar
    for h in range(4):
        for c in range(2):
            blocks[("mq", h, c)] = o_mq + h * 256 + c * 128 + ar
            blocks[("zm", h, c)] = o_zm + h * 256 + c * 128 + ar
    for dc in range(D // 128):
        blocks[("ga", dc)] = o_ga + dc * 128 + ar
        blocks[("gb", dc)] = o_gb + dc * 128 + ar
        blocks[("gm", dc)] = o_gm + dc * 128 + ar
    return blocks


def kv_tiles(seg, grp):
    d, H = GRP[grp]
    L = SEG_CORE[seg] + 2 * H
    out = []
    i = 0
    while i < L:
        n = min(512, L - i)
        out.append((i, n))
        i += n
    return d, H, L, L // d, out


def build_program(D):
    KC = D // 128
    WS = max(KC, MIXC) * 128
    G4 = min(4, KC)
    blocks = in_col_blocks(D)
    bnames = list(blocks.keys())
    bidx = {n: i for i, n in enumerate(bnames)}
    NBIN = len(bnames)

    nc = bass.Bass("TRN2", target_bir_lowering=False)
    dt_in = lambda name, shape: nc.dram_tensor(name, list(shape), F32, kind="ExternalInput").ap()
    X = {s: dt_in("x" + s, [SEG_CORE[s] + 2 * MAXH + 16, D]) for s in "PS"}
    MEM = {s: dt_in("mem" + s, [256, D]) for s in "PS"}
    GN = dt_in("gn", [1, D])
    GM = dt_in("gm", [1, D])
    GF = dt_in("gf", [1, D])
    SINK = dt_in("sink", [1, 16])
    WIN = dt_in("win", [NBIN, 128, KC * 128])
    WMEM = dt_in("wmem", [16, 128, KC * 128])
    WBR = dt_in("wbr", [KC, 128, MIXC * 128])
    WOUT = dt_in("wout", [KC, 128, KC * 128])
    COSQ = {s: dt_in("cosq" + s, [128, SEG_CORE[s]]) for s in "PS"}
    SINQ = {s: dt_in("sinq" + s, [128, SEG_CORE[s]]) for s in "PS"}
    COSK = {}
    SINK_ = {}
    for s in "PS":
        for g in GRP:
            L = SEG_CORE[s] + 2 * GRP[g][1]
            COSK[s, g] = dt_in("cosk%s%s" % (s, g), [128, L])
            SINK_[s, g] = dt_in("sink%s%s" % (s, g), [128, L])
    VALW = {s: dt_in("valw" + s, [SEG_CORE[s] // TQ, 128, NVAL]) for s in "PS"}
    CONST = dt_in("const", [128, 3 * 128 + 6 * 256])
    Y = {s: nc.dram_tensor("y" + s, [SEG_CORE[s], D], F32, kind="ExternalOutput").ap() for s in "PS"}

    dint = lambda name, shape: nc.dram_tensor(name, list(shape), BF16, kind="Internal").ap()
    WSCR = {"win": dint("s_win", [NBIN, 128, KC * 128]), "wmem": dint("s_wmem", [16, 128, KC * 128]),
            "wbr": dint("s_wbr", [KC, 128, MIXC * 128]), "wout": dint("s_wout", [KC, 128, KC * 128])}
    WSRC = {"win": WIN, "wmem": WMEM, "wbr": WBR, "wout": WOUT}
    KTC = {}
    VC = {}
    for s in "PS":
        for gi, g in enumerate(("n", "d4", "d16")):
            L = SEG_CORE[s] + 2 * GRP[g][1]
            KTC[s, "a", gi] = dint("kt_%s_a%d" % (s, gi), [8, 128, L])
            VC[s, "a", gi] = dint("v_%s_a%d" % (s, gi), [L, 1024])
        L = SEG_CORE[s] + 2 * GRP["n"][1]
        KTC[s, "b", 0] = dint("kt_%s_b" % s, [4, 128, L])
        VC[s, "b", 0] = dint("v_%s_b" % s, [L, 512])

    st = ExitStack()
    with st:
        P = Prog(nc, st)
        sbt = lambda name, shape, dt: st.enter_context(nc.sbuf_tensor(name, list(shape), dt))
        PS = [st.enter_context(nc.psum_tensor("psb%d" % i, [128, 512], F32)) for i in range(8)]
        BK_G = (0, 1)
        BK_G4 = (0, 1, 3, 4)
        BK_ROT = 2
        BK_S = (3, 4)
        BK_ACC = (5, 6, 7)
        cnt = {"g": 0, "s": 0, "g4": 0}

        def gbank():
            b = BK_G[cnt["g"] % 2]
            cnt["g"] += 1
            return b

        def gbank4():
            b = BK_G4[cnt["g4"] % 4]
            cnt["g4"] += 1
            return b

        def sbank():
            b = BK_S[cnt["s"] % 2]
            cnt["s"] += 1
            return b

        ACTB = sbt("actb", [128, KC * 256 + MIXC * 256 + KC * 256], BF16)
        hT = ACTB[:, 0:KC * 256].rearrange("p (k n) -> p k n", k=KC)
        bzT = ACTB[:, KC * 256:KC * 256 + MIXC * 256].rearrange("p (k n) -> p k n", k=MIXC)
        uT = ACTB[:, KC * 256 + MIXC * 256:].rearrange("p (k n) -> p k n", k=KC)
        hTkv = ACTB[:, 0:KC * 512].rearrange("p (k n) -> p k n", k=KC)
        xo = [sbt("xo%d" % i, [128, D], F32) for i in range(2)]
        hbs = [sbt("hb%d" % i, [128, D], BF16) for i in range(2)]
        stat2 = sbt("stat2", [128, 8], F32)
        fcnt = {"n": 0}
        gt = sbt("gt", [128, D], F32)
        NSLOT = 4
        WSL = sbt("wsl", [128, NSLOT, WS], BF16)
        cst = sbt("cst", [128, 3 * 128 + 6 * 256], BF16)
        ident = cst[:, 0:128]
        rotm = cst[:, 128:256]
        ones = cst[:, 256:384]
        masks = cst[:, 384:].rearrange("p (m n) -> p m n", m=6)
        sinkt = sbt("sinkt", [128, 16], F32)
        esink = sbt("esink", [128, 16], F32)
        stat = sbt("stat", [128, 16], F32)
        cosb = sbt("cosb", [128, 512], F32)
        sinb = sbt("sinb", [128, 512], F32)
        qbf = [sbt("qbf%d" % i, [128, 512], BF16) for i in range(2)]
        t1b = [sbt("t1b%d" % i, [128, 512], F32) for i in range(1)]
        t2b = [sbt("t2b%d" % i, [128, 512], F32) for i in range(1)]
        kout = [sbt("kout%d" % i, [128, 512], BF16) for i in range(2)]
        vout = [sbt("vout%d" % i, [128, 256], BF16) for i in range(2)]
        qT = [sbt("qT%d" % i, [128, 256], BF16) for i in range(8)]
        zs = [sbt("zs%d" % i, [128, 256], F32) for i in range(2)]
        Eb = [sbt("Eb%d" % i, [128, 512], BF16) for i in range(2)]
        PTb = [sbt("PTb%d" % i, [128, 512], BF16) for i in range(3)]
        rden = sbt("rden", [128, 256], F32)
        tnum = sbt("tnum", [128, 256], F32)
        tsum = sbt("tsum", [128, 256], F32)
        gs = [sbt("gs%d" % i, [128, 256], F32) for i in range(3)]
        uacc = sbt("uacc", [128, 256], F32)
        utmp = sbt("utmp", [128, 256], F32)
        valw = sbt("valw", [128, NVAL], BF16)
        KmT = sbt("KmT", [128, 8, 256], BF16)
        Vm = sbt("Vm", [128, 2, 1024], BF16)
        k1 = sbt("k1", [128, 384], BF16)
        k2 = sbt("k2", [128, 4, 192], BF16)
        k3 = sbt("k3", [128, 16, 144], BF16)
        v1 = sbt("v1", [128, 3, 128], BF16)
        v2lo = sbt("v2lo", [128, 4, 128], BF16)
        v2hi = sbt("v2hi", [128, 4, 128], BF16)
        v3lo = sbt("v3lo", [128, 16, 128], BF16)
        v3hi = sbt("v3hi", [128, 16, 128], BF16)
        kbw = sbt("kbw", [128, 512], BF16)
        vbw = sbt("vbw", [128, 4, 128], BF16)
        rr = {"q": 0, "t": 0, "k": 0, "v": 0, "qT": 0, "z": 0, "E": 0, "PT": 0}

        def rot(name, lst):
            i = rr[name] % len(lst)
            rr[name] += 1
            return i, lst[i]

        plan = []
        wstate = {"n": 0, "issued": 0, "cast": set(), "ncast": 0, "castptr": 0}

        def cast_ahead(mm):
            return min(160, 12 + mm // 2)

        def w_issue(m):
            mm = wstate["castptr"]
            while mm < len(plan) and mm - cast_ahead(mm) < m:
                if plan[mm] is not None:
                    f2, b2 = plan[mm]
                    if (f2, b2) not in wstate["cast"]:
                        wstate["cast"].add((f2, b2))
                        dep = mm - cast_ahead(mm)
                        while dep >= 0 and plan[dep] is None:
                            dep -= 1
                        P.dma("pool", lambda e, f2=f2, b2=b2: [e.dma_start(out=WSCR[f2][b2], in_=WSRC[f2][b2])],
                              ("wcast", wstate["ncast"] % 8), reads=([("wl", dep)] if dep >= 0 else []),
                              writes=[("wscr", f2, b2)])
                        wstate["ncast"] += 1
                mm += 1
            wstate["castptr"] = mm
            if plan[m] is None:
                return
            fam, blk = plan[m]
            assert (fam, blk) in wstate["cast"]
            slot = m % NSLOT
            n_el = (MIXC if fam == "wbr" else KC) * 128
            P.dma("sp", lambda e, fam=fam, blk=blk, slot=slot, n_el=n_el:
                  [e.dma_start(out=WSL[:, slot, 0:n_el], in_=WSCR[fam][blk])],
                  ("w", slot), reads=[("wscr", fam, blk)], writes=[("w", slot), ("wl", m)])

        def _req(entry):
            n = wstate["n"]
            wstate["n"] += 1
            if P.dry:
                plan.append(entry)
            else:
                assert plan[n] == entry, (n, plan[n], entry)
                while wstate["issued"] < min(n + NSLOT - 1, len(plan)):
                    w_issue(wstate["issued"])
                    wstate["issued"] += 1
                if wstate["issued"] <= n:
                    w_issue(n)
                    wstate["issued"] = n + 1
            return n % NSLOT

        def get_w(fam, blk):
            slot = _req((fam, blk))
            kk = MIXC if fam == "wbr" else KC
            return WSL[:, slot, 0:kk * 128].rearrange("p (k c) -> p k c", k=kk), ("w", slot)

        def get_w_pair(fam, blk):
            if wstate["n"] % 2 == 1:
                _req(None)
            s0 = _req((fam, blk))
            s1 = _req((fam, blk + 1))
            assert s1 == s0 + 1
            return (WSL[:, s0:s0 + 2, 0:KC * 128].rearrange("p s (k c) -> p s k c", k=KC),
                    [("w", s0), ("w", s1)])

        def load_consts():
            P.dma("pool", lambda e: [e.dma_start(out=cst[:], in_=CONST)], "cst", writes=["cst"])
            P.dma("sp", lambda e: [e.dma_start(out=sinkt[:], in_=SINK.partition_broadcast(128))], "sinkt",
                  writes=["sinkt"])
            P.op("act", lambda e: e.activation(out=esink[:], in_=sinkt[:], func=AF.Exp), reads=["sinkt"],
                 writes=["esink"])

        def load_g(which):
            src = {"gn": GN, "gm": GM, "gf": GF}[which]
            P.dma("sp", lambda e: [e.dma_start(out=gt[:], in_=src.partition_broadcast(128))], "gt", writes=["gt"])

        def front_a(xsrc, runs, b):
            xb = xo[b]
            par = fcnt["n"] % 2
            fcnt["n"] += 1
            hb = hbs[par]
            hk = ("hb", par)
            sk = ("stat", par)
            so = 4 * par

            def ld(e):
                r = []
                for (p0, cn, row0, rs) in runs:
                    if rs == 1:
                        src = xsrc[row0:row0 + cn, :]
                    else:
                        src = xsrc[row0:row0 + cn * rs, :].rearrange("(i s) d -> i s d", s=rs)[:, 0, :]
                    r.append(e.dma_start(out=xb[p0:p0 + cn, :], in_=src))
                return r

            P.dma("sp", ld, ("xo", b), writes=[("xo", b)], n=len(runs))
            P.op("act", lambda e: e.activation(out=hb[:], in_=xb[:], func=AF.Square, accum_out=stat[:, so:so + 1]),
                 reads=[("xo", b)], writes=[hk, sk])
            P.op("dve", lambda e: e.tensor_scalar(out=stat[:, so + 1:so + 2], in0=stat[:, so:so + 1], scalar1=1.0 / D,
                                                  scalar2=1e-6, op0=ALU.mult, op1=ALU.add), reads=[sk], writes=[sk])
            P.op("act", lambda e: e.activation(out=stat[:, so + 2:so + 3], in_=stat[:, so + 1:so + 2], func=AF.Sqrt),
                 reads=[sk], writes=[sk])
            P.op("dve", lambda e: e.reciprocal(out=stat[:, so + 3:so + 4], in_=stat[:, so + 2:so + 3]), reads=[sk],
                 writes=[sk])
            P.op("dve", lambda e: e.scalar_tensor_tensor(out=hb[:], in0=xb[:], scalar=stat[:, so + 3:so + 4], in1=gt[:],
                                                         op0=ALU.mult, op1=ALU.mult),
                 reads=[("xo", b), sk, "gt"], writes=[hk])
            return (hb, hk)

        def front_b(hnd, dst, dstkeyf, col0, bankf):
            hb, hk = hnd
            for q in range(KC // G4):
                bk = bankf()
                pt = PS[bk][:, :].bitcast(BF16)
                for i in range(G4):
                    kc = q * G4 + i
                    P.op("pe", lambda e, pt=pt, i=i, kc=kc: e.transpose(out=pt[:, i * 128:(i + 1) * 128],
                                                                       in_=hb[:, kc * 128:(kc + 1) * 128],
                                                                       identity=ident),
                         reads=[hk, "cst"], writes=[("ps", bk)])
                src = pt[:, 0:G4 * 128].rearrange("p (a b) -> p a b", a=G4)
                dv = dst[:, q * G4:(q + 1) * G4, col0:col0 + 128]
                if q % 2 == 0:
                    P.op("act", lambda e, src=src, dv=dv: e.activation(out=dv, in_=src, func=AF.Copy),
                         reads=[("ps", bk)], writes=dstkeyf(q))
                else:
                    P.op("dve", lambda e, src=src, dv=dv: e.tensor_copy(out=dv, in_=src),
                         reads=[("ps", bk)], writes=dstkeyf(q))

        def front_pipe(specs, dst, dstkeyf, bankf, pre=None, nxt=None):
            hs = list(pre) if pre else []
            n = len(specs)
            while len(hs) < min(2, n):
                xs_, runs, b, _ = specs[len(hs)]
                hs.append(front_a(xs_, runs, b))
            for i in range(n):
                front_b(hs[i], dst, dstkeyf, specs[i][3], bankf)
                if i + 2 < n:
                    xs_, runs, b, _ = specs[i + 2]
                    hs.append(front_a(xs_, runs, b))
            out = []
            if nxt:
                for j in range(min(2, len(nxt))):
                    xs_, runs, b, _ = nxt[j]
                    out.append(front_a(xs_, runs, b))
            return out

        R = lambda j: ("R", j)
        hq_keys = lambda q: [R(j) for j in range(q * G4, (q + 1) * G4)]
        kvq_keys = lambda q: [R(j) for j in range(2 * q * G4, 2 * (q + 1) * G4)]
        h_kc = lambda kc: [R(kc)]
        kv_kc = lambda kc: [R(2 * kc), R(2 * kc + 1)]

        def gemm_fm(w, wkey, src3, srckeys, ntok, bk):
            pt = PS[bk]
            for kc in range(KC):
                P.op("pe", lambda e, kc=kc: e.matmul(pt[:, 0:ntok], w[:, kc, :], src3[:, kc, 0:ntok],
                                                     start=(kc == 0), stop=(kc == KC - 1)),
                     reads=[wkey] + srckeys(kc), writes=[("ps", bk)])
            return pt

        def gemm_q(name):
            w, wk = get_w("win", bidx[name])
            bk = gbank()
            gemm_fm(w, wk, hT, h_kc, TQ, bk)
            return bk

        def rope_a(bk, n):
            qi, qb_ = rot("q", qbf)
            P.op("act", lambda e: e.activation(out=qb_[:, 0:n], in_=PS[bk][:, 0:n], func=AF.Copy),
                 reads=[("ps", bk)], writes=[("qbf", qi)])
            return (bk, n, qi, qb_)

        def rope_b(hnd, dest, destkeys):
            bk, n, qi, qb_ = hnd
            pt = PS[bk]
            ti, t1 = 0, t1b[0]
            t2 = t2b[0]
            rp = PS[BK_ROT]
            P.op("pe", lambda e: e.matmul(rp[:, 0:n], rotm, qb_[:, 0:n], start=True, stop=True),
                 reads=[("qbf", qi), "cst"], writes=[("ps", BK_ROT)])
            P.op("dve", lambda e: e.tensor_tensor(out=t1[:, 0:n], in0=pt[:, 0:n], in1=cosb[:, 0:n], op=ALU.mult),
                 reads=[("ps", bk), "cos"], writes=[("t1", ti)])
            P.op("dve", lambda e: e.tensor_tensor(out=t2[:, 0:n], in0=rp[:, 0:n], in1=sinb[:, 0:n], op=ALU.mult),
                 reads=[("ps", BK_ROT), "sin"], writes=["t2"])
            P.op("dve", lambda e: e.tensor_tensor(out=dest, in0=t1[:, 0:n], in1=t2[:, 0:n], op=ALU.add),
                 reads=[("t1", ti), "t2"], writes=destkeys)

        def phase_m(seg):
            load_g("gm")
            front_pipe([(MEM[seg], [(0, 128, sub * 128, 1)], sub, sub * 128) for sub in range(2)], hT, hq_keys, gbank)
            for h in range(4):
                for c in range(2):
                    w, wk = get_w("wmem", h * 2 + c)
                    bk = gbank()
                    pt = gemm_fm(w, wk, hT, h_kc, 256, bk)
                    P.op("act", lambda e, pt=pt, h=h, c=c: e.activation(out=KmT[:, h * 2 + c, :], in_=pt[:, 0:256],
                                                                        func=AF.Copy),
                         reads=[("ps", bk)], writes=["KmT"])
            for h in range(4):
                w4, wks = get_w_pair("wmem", 8 + 2 * h)
                for sub in range(2):
                    bk = gbank()
                    pv = PS[bk][:, 0:256].rearrange("p (s c) -> p s c", s=2)
                    for kc in range(KC):
                        P.op("pe", lambda e, pv=pv, kc=kc, sub=sub, w4=w4: e.matmul(
                            pv, hT[:, kc, sub * 128:(sub + 1) * 128], w4[:, :, kc, :],
                            start=(kc == 0), stop=(kc == KC - 1)), reads=wks + h_kc(kc), writes=[("ps", bk)])
                    P.op("dve", lambda e, bk=bk, h=h, sub=sub: e.tensor_copy(out=Vm[:, sub, h * 256:(h + 1) * 256],
                                                                             in_=PS[bk][:, 0:256]),
                         reads=[("ps", bk)], writes=["Vm"])

        cw_keys = {}

        def phase_kv(seg, grp):
            d, H, L, npc, tiles = kv_tiles(seg, grp)
            base = MAXH - H
            gi = {"n": 0, "d4": 1, "d16": 2}[grp]
            kblocks = [(("ak", gi, h), KTC[seg, "a", gi], h) for h in range(8)]
            vblocks = [(("av", gi, 2 * hp), VC[seg, "a", gi], hp * 256) for hp in range(4)]
            if grp == "n":
                kblocks += [(("bk", h), KTC[seg, "b", 0], h) for h in range(4)]
                vblocks += [(("bv", 2 * hp), VC[seg, "b", 0], hp * 256) for hp in range(2)]
            keys = cw_keys.setdefault((seg, grp), [])
            load_g("gn")
            def tile_specs(i0, nt):
                sp_ = []
                for sub in range(nt // 128):
                    runs = []
                    idx = i0 + sub * 128
                    p0 = 0
                    while p0 < 128:
                        r, j = divmod(idx + p0, npc)
                        cn = min(128 - p0, npc - j)
                        runs.append((p0, cn, base + r + d * j, d))
                        p0 += cn
                    sp_.append((X[seg], runs, sub % 2, sub * 128))
                return sp_

            pre = None
            for ti_, (i0, nt) in enumerate(tiles):
                nsub = nt // 128
                nxt_specs = tile_specs(*tiles[ti_ + 1]) if ti_ + 1 < len(tiles) else None
                pre = front_pipe(tile_specs(i0, nt), hTkv, kvq_keys, gbank4, pre=pre, nxt=nxt_specs)
                P.dma("sp", lambda e, i0=i0, nt=nt: [e.dma_start(out=cosb[:, 0:nt], in_=COSK[seg, grp][:, i0:i0 + nt])],
                      "cos", writes=["cos"])
                P.dma("sp", lambda e, i0=i0, nt=nt: [e.dma_start(out=sinb[:, 0:nt], in_=SINK_[seg, grp][:, i0:i0 + nt])],
                      "sin", writes=["sin"])

                def k_finish(pend):
                    hnd, cache, h = pend
                    ki, ko = rot("k", kout)
                    rope_b(hnd, ko[:, 0:nt], [("kout", ki)])
                    key = ("cw", seg, grp, len(keys))
                    keys.append(key)
                    P.dma("sp", lambda e, ko=ko, cache=cache, h=h, i0=i0, nt=nt:
                          [e.dma_start(out=cache[h][:, i0:i0 + nt], in_=ko[:, 0:nt])],
                          ("kout", ki), reads=[("kout", ki)], writes=[key])

                pend = None
                for (bn, cache, h) in kblocks:
                    w, wk = get_w("win", bidx[bn])
                    bk = gbank4()
                    gemm_fm(w, wk, hTkv, kv_kc, nt, bk)
                    hnd = rope_a(bk, nt)
                    if pend is not None:
                        k_finish(pend)
                    pend = (hnd, cache, h)
                k_finish(pend)
                for (bn, cache, c0) in vblocks:
                    w4, wks = get_w_pair("win", bidx[bn])
                    for sub in range(nsub):
                        bk = gbank4()
                        pv = PS[bk][:, 0:256].rearrange("p (s c) -> p s c", s=2)
                        for kc in range(KC):
                            P.op("pe", lambda e, pv=pv, kc=kc, sub=sub, w4=w4: e.matmul(
                                pv, hTkv[:, kc, sub * 128:(sub + 1) * 128], w4[:, :, kc, :],
                                start=(kc == 0), stop=(kc == KC - 1)), reads=wks + kv_kc(kc), writes=[("ps", bk)])
                        vi, vo = rot("v", vout)
                        if sub % 2 == 0:
                            P.op("act", lambda e, bk=bk, vo=vo: e.activation(out=vo[:], in_=PS[bk][:, 0:256], func=AF.Copy),
                                 reads=[("ps", bk)], writes=[("vout", vi)])
                        else:
                            P.op("dve", lambda e, bk=bk, vo=vo: e.tensor_copy(out=vo[:], in_=PS[bk][:, 0:256]),
                                 reads=[("ps", bk)], writes=[("vout", vi)])
                        key = ("cw", seg, grp, len(keys))
                        keys.append(key)
                        r0 = i0 + sub * 128
                        P.dma("sp", lambda e, vo=vo, cache=cache, r0=r0, c0=c0:
                              [e.dma_start(out=cache[r0:r0 + 128, c0:c0 + 256], in_=vo[:])],
                              ("vout", vi), reads=[("vout", vi)], writes=[key])
            P.op("sp", lambda e: e.nop(), reads=list(keys), writes=[("cache", seg, grp)])

        def q_rope_finish(hnd):
            qi, qt = rot("qT", qT)
            rope_b(hnd, qt[:, :], [("qT", qi)])
            return qi, qt

        def evac_q(bk):
            qi, qt = rot("qT", qT)
            P.op("act", lambda e: e.activation(out=qt[:, :], in_=PS[bk][:, 0:TQ], func=AF.Copy),
                 reads=[("ps", bk)], writes=[("qT", qi)])
            return qi, qt

        def silu_z(bk):
            zi, z = rot("z", zs)
            P.op("act", lambda e: e.activation(out=z[:], in_=PS[bk][:, 0:TQ], func=AF.Silu),
                 reads=[("ps", bk)], writes=[("zs", zi)])
            return zi, z

        def score_exp(mms, rows_lo, rows_hi, scale, mask_lo, mask_hi, extra_reads):
            bk = sbank()
            pt = PS[bk]
            for (half, c0, ncol, nrow, lhs, rhs) in mms:
                for i, (l, r) in enumerate(zip(lhs, rhs)):
                    P.op("pe", lambda e, half=half, c0=c0, ncol=ncol, nrow=nrow, l=l, r=r, i=i, nl=len(lhs):
                         e.matmul(pt[0:nrow, half * 256 + c0:half * 256 + c0 + ncol], l, r, start=(i == 0),
                                  stop=(i == nl - 1)),
                         reads=extra_reads, writes=[("ps", bk)])
            ei, E = rot("E", Eb)
            pi, PT = rot("PT", PTb)
            for half, rows, mask in ((0, rows_lo, mask_lo), (1, rows_hi, mask_hi)):
                if rows == 0:
                    continue
                sl = slice(half * 256, half * 256 + 256)
                if mask is None:
                    P.op("act", lambda e, rows=rows, sl=sl: e.activation(out=PT[0:rows, sl], in_=pt[0:rows, sl],
                                                                         func=AF.Exp, scale=scale),
                         reads=[("ps", bk)], writes=[("PT", pi)])
                else:
                    P.op("act", lambda e, rows=rows, sl=sl: e.activation(out=E[0:rows, sl], in_=pt[0:rows, sl],
                                                                         func=AF.Exp, scale=scale),
                         reads=[("ps", bk)], writes=[("E", ei)])
                    P.op("dve", lambda e, rows=rows, sl=sl, mask=mask: e.tensor_tensor(
                        out=PT[0:rows, sl], in0=E[0:rows, sl], in1=mask[0:rows, :], op=ALU.mult),
                        reads=[("E", ei), "cst"], writes=[("PT", pi)])
            return pi, PT

        def finish_head(parts, extra_den, z, zi, dest, destkey):
            bks = [("ps", b) for b in BK_ACC]

            def vw(ps_ap, sb_tile, U):
                if U is None:
                    return ps_ap, sb_tile[:]
                return (ps_ap.rearrange("p (u j) -> p j u", u=U), sb_tile[:].rearrange("p (j u) -> p j u", u=U))

            acc0, den0, _ = parts[0]
            if extra_den is not None:
                P.op("dve", lambda e: e.tensor_scalar_add(out=tsum[:], in0=den0, scalar1=extra_den),
                     reads=bks + ["esink"], writes=["tsum"])
            else:
                P.op("dve", lambda e: e.tensor_copy(out=tsum[:], in_=den0), reads=bks, writes=["tsum"])
            for (_, dn, U) in parts[1:]:
                pv, sv = vw(dn, tsum, U)
                P.op("dve", lambda e, pv=pv, sv=sv: e.tensor_tensor(out=sv, in0=pv, in1=sv, op=ALU.add),
                     reads=bks + ["tsum"], writes=["tsum"])
            P.op("dve", lambda e: e.reciprocal(out=rden[:], in_=tsum[:]), reads=["tsum"], writes=["rden"])
            if len(parts) == 1:
                P.op("dve", lambda e: e.tensor_tensor(out=tnum[:], in0=acc0, in1=rden[:], op=ALU.mult),
                     reads=bks + ["rden"], writes=["tnum"])
            else:
                P.op("dve", lambda e: e.tensor_copy(out=tnum[:], in_=acc0), reads=bks, writes=["tnum"])
                for (ac, _, U) in parts[1:]:
                    pv, sv = vw(ac, tnum, U)
                    P.op("dve", lambda e, pv=pv, sv=sv: e.tensor_tensor(out=sv, in0=pv, in1=sv, op=ALU.add),
                         reads=bks + ["tnum"], writes=["tnum"])
                P.op("dve", lambda e: e.tensor_tensor(out=tnum[:], in0=tnum[:], in1=rden[:], op=ALU.mult),
                     reads=["tnum", "rden"], writes=["tnum"])
            P.op("dve", lambda e: e.tensor_tensor(out=dest, in0=tnum[:], in1=z[:], op=ALU.mult),
                 reads=["tnum", ("zs", zi)], writes=[destkey])

        def phase_q(seg):
            core = SEG_CORE[seg]
            nt_ = core // TQ
            sc = float(1.0 / np.sqrt(128.0))
            scm = 1.0 / 16.0
            c1, c2, c3 = KTC[seg, "a", 0], KTC[seg, "a", 1], KTC[seg, "a", 2]
            V1, V2, V3 = VC[seg, "a", 0], VC[seg, "a", 1], VC[seg, "a", 2]
            ck = [("cache", seg, g) for g in ("n", "d4", "d16")]
            A0, A1, A2 = (PS[b] for b in BK_ACC)
            acc = [A0[:, 0:256], A0[:, 256:512], A1[:, 0:256]]
            den = [A1[:, 256:512], A2[:, 0:256], A2[:, 256:512]]
            acck = [("ps", BK_ACC[0]), ("ps", BK_ACC[0]), ("ps", BK_ACC[1])]
            denk = [("ps", BK_ACC[1]), ("ps", BK_ACC[2]), ("ps", BK_ACC[2])]

            def val(col, rows):
                return valw[0:rows, col:col + 1].to_broadcast([rows, 128])

            for t in range(min(nt_, DEBUG_QT)):
                tok0 = t * TQ
                load_g("gn")
                front_pipe([(X[seg], [(0, 128, MAXH + tok0 + sub * 128, 1)], sub, sub * 128) for sub in range(2)],
                           hT, hq_keys, gbank)
                P.dma("sp", lambda e, tok0=tok0: [e.dma_start(out=cosb[:, 0:TQ], in_=COSQ[seg][:, tok0:tok0 + TQ])],
                      "cos", writes=["cos"])
                P.dma("sp", lambda e, tok0=tok0: [e.dma_start(out=sinb[:, 0:TQ], in_=SINQ[seg][:, tok0:tok0 + TQ])],
                      "sin", writes=["sin"])
                P.dma("pool", lambda e, t=t: [e.dma_start(out=valw[:], in_=VALW[seg][t])], "valw", writes=["valw"])

                def a_proj1(s):
                    hs = []
                    qs = []
                    for g in range(2):
                        bk = gemm_q(("qa", s, g))
                        hs.append(rope_a(bk, TQ))
                        if g >= 1:
                            qs.append(q_rope_finish(hs[g - 1]))
                    return hs, qs

                def a_proj2(s, part):
                    hs, qs = part
                    bk = gemm_q(("qa", s, 2))
                    hs.append(rope_a(bk, TQ))
                    qs.append(q_rope_finish(hs[1]))
                    bk = gemm_q(("za", s))
                    qs.append(q_rope_finish(hs[2]))
                    z = silu_z(bk)
                    return qs, z

                def a_win(s):
                    cs_ = slice(s * 128, (s + 1) * 128)
                    P.dma("sp", lambda e, s=s, tok0=tok0, cs_=cs_: [
                        e.dma_start(out=k1[:], in_=c1[s][:, tok0 + 64:tok0 + 448]),
                        e.dma_start(out=v1[:], in_=V1[tok0 + 64:tok0 + 448, cs_].rearrange("(c p) d -> p c d", p=128)),
                    ], "win1", reads=[ck[0]], writes=["k1", "v1"], n=2)
                    j4 = tok0 // 4
                    P.dma("sp", lambda e, s=s, j4=j4, cs_=cs_: [
                        e.dma_start(out=k2[:], in_=c2[s].rearrange("p (u n) -> p u n", u=4)[:, :, j4:j4 + 192]),
                        e.dma_start(out=v2lo[:], in_=V2.rearrange("(u n) d -> n u d", u=4)[j4:j4 + 128, :, cs_]),
                        e.dma_start(out=v2hi[0:64], in_=V2.rearrange("(u n) d -> n u d", u=4)[j4 + 128:j4 + 192, :, cs_]),
                    ], "win2", reads=[ck[1]], writes=["k2", "v2"], n=3)
                    j16 = tok0 // 16
                    P.dma("sp", lambda e, s=s, j16=j16, cs_=cs_: [
                        e.dma_start(out=k3[:], in_=c3[s].rearrange("p (u n) -> p u n", u=16)[:, :, j16:j16 + 144]),
                        e.dma_start(out=v3lo[:], in_=V3.rearrange("(u n) d -> n u d", u=16)[j16:j16 + 128, :, cs_]),
                        e.dma_start(out=v3hi[0:16], in_=V3.rearrange("(u n) d -> n u d", u=16)[j16 + 128:j16 + 144, :, cs_]),
                    ], "win3", reads=[ck[2]], writes=["k3", "v3"], n=3)

                def a_geom(g, qt):
                    if g == 0:
                        return dict(U=2, nq=128, klo=lambda u: k1[:, u * 128:(u + 1) * 128],
                                    khi=lambda u: k1[:, (u + 1) * 128:(u + 2) * 128],
                                    vlo=lambda u: v1[:, u, :], vhi=lambda u: v1[:, u + 1, :],
                                    vallo=lambda u: val(0 + u, 128), valhi=lambda u: val(1 + u, 128),
                                    qcol=(lambda u: qt[:, u * 128:(u + 1) * 128]) if qt is not None else None,
                                    rk=["k1", "v1"])
                    if g == 1:
                        return dict(U=4, nq=64, klo=lambda u: k2[:, u, 0:128], khi=lambda u: k2[:, u, 128:192],
                                    vlo=lambda u: v2lo[:, u, :], vhi=lambda u: v2hi[0:64, u, :],
                                    vallo=lambda u: val(3 + u, 128), valhi=lambda u: val(7 + u, 64),
                                    qcol=(lambda u: qt[:, :].rearrange("p (j u) -> p u j", u=4)[:, u, :]) if qt is not None else None,
                                    rk=["k2", "v2"])
                    return dict(U=16, nq=16, klo=lambda u: k3[:, u, 0:128], khi=lambda u: k3[:, u, 128:144],
                                vlo=lambda u: v3lo[:, u, :], vhi=lambda u: v3hi[0:16, u, :],
                                vallo=lambda u: val(11 + u, 128), valhi=lambda u: val(27 + u, 16),
                                qcol=(lambda u: qt[:, :].rearrange("p (j u) -> p u j", u=16)[:, u, :]) if qt is not None else None,
                                rk=["k3", "v3"])

                def a_scores(qs):
                    pts = []
                    for g in range(3):
                        qi, qt = qs[g]
                        G = a_geom(g, qt)
                        U, nq = G["U"], G["nq"]
                        mms = []
                        for u in range(U):
                            mms.append((0, u * nq, nq, 128, [G["klo"](u)], [G["qcol"](u)]))
                            mms.append((1, u * nq, nq, nq, [G["khi"](u)], [G["qcol"](u)]))
                        pts.append(score_exp(mms, 128, nq, sc, masks[:, 2 * g, :], masks[:, 2 * g + 1, :],
                                             [("qT", qi)] + G["rk"]))
                    return pts

                def a_pv(pts):
                    for g in range(3):
                        pi, PT = pts[g]
                        G = a_geom(g, None)
                        U, nq, rk = G["U"], G["nq"], G["rk"]
                        for u in range(U):
                            cs2 = slice(u * nq, (u + 1) * nq)
                            cs2h = slice(256 + u * nq, 256 + (u + 1) * nq)
                            vlo, vhi, vallo, valhi = G["vlo"](u), G["vhi"](u), G["vallo"](u), G["valhi"](u)
                            P.op("pe", lambda e, g=g, cs2=cs2, vlo=vlo, PT=PT: e.matmul(
                                acc[g][:, cs2], vlo, PT[:, cs2], start=True, stop=False),
                                reads=[("PT", pi)] + rk, writes=[acck[g]])
                            P.op("pe", lambda e, g=g, cs2=cs2, cs2h=cs2h, vhi=vhi, PT=PT, nq=nq: e.matmul(
                                acc[g][:, cs2], vhi, PT[0:nq, cs2h], start=False, stop=True),
                                reads=[("PT", pi)] + rk, writes=[acck[g]])
                            P.op("pe", lambda e, g=g, cs2=cs2, vallo=vallo, PT=PT: e.matmul(
                                den[g][:, cs2], vallo, PT[:, cs2], start=True, stop=False),
                                reads=[("PT", pi), "valw"], writes=[denk[g]])
                            P.op("pe", lambda e, g=g, cs2=cs2, cs2h=cs2h, valhi=valhi, PT=PT, nq=nq: e.matmul(
                                den[g][:, cs2], valhi, PT[0:nq, cs2h], start=False, stop=True),
                                reads=[("PT", pi), "valw"], writes=[denk[g]])

                st_ = a_proj2(0, a_proj1(0))
                a_win(0)
                for s in range(8):
                    qs, (zi, z) = st_
                    if s < 7:
                        part = a_proj1(s + 1)
                    pts = a_scores(qs)
                    if s < 7:
                        nxt = a_proj2(s + 1, part)
                    a_pv(pts)
                    if s < 7:
                        a_win(s + 1)
                    parts = [(acc[0], den[0], None), (acc[1], den[1], 4), (acc[2], den[2], 16)]
                    finish_head(parts, None, z, zi, bzT[:, s, :], R(KC + s))
                    if s < 7:
                        st_ = nxt

                def b_win(j):
                    P.dma("sp", lambda e, j=j, tok0=tok0: [
                        e.dma_start(out=kbw[:], in_=KTC[seg, "b", 0][j][:, tok0:tok0 + 512]),
                        e.dma_start(out=vbw[:], in_=VC[seg, "b", 0][tok0:tok0 + 512, j * 128:(j + 1) * 128]
                                    .rearrange("(c p) d -> p c d", p=128)),
                    ], "winb", reads=[("cache", seg, "n")], writes=["kb", "vb"], n=2)

                def b_q(h):
                    bk = gemm_q(("qb", h))
                    return rope_a(bk, TQ)

                qcur = []
                hnd = None
                for hq in range(4):
                    h2 = b_q(hq)
                    if hnd is not None:
                        qcur.append(q_rope_finish(hnd))
                    hnd = h2
                qcur.append(q_rope_finish(hnd))
                b_win(0)
                A0b, A1b = PS[BK_ACC[0]], PS[BK_ACC[1]]
                for j in range(4):
                    qnext = []
                    for hq in range(4):
                        h = 4 * j + hq
                        qi, qt = qcur[hq]
                        mm_pn, mm_c = [], []
                        for sb in range(2):
                            qc = qt[:, sb * 128:(sb + 1) * 128]
                            mm_pn.append((0, sb * 128, 128, 128, [kbw[:, sb * 128:(sb + 1) * 128]], [qc]))
                            mm_pn.append((1, sb * 128, 128, 128, [kbw[:, (sb + 2) * 128:(sb + 3) * 128]], [qc]))
                            mm_c.append((0, sb * 128, 128, 128, [kbw[:, (sb + 1) * 128:(sb + 2) * 128]], [qc]))
                        p1, PT1 = score_exp(mm_pn, 128, 128, sc, masks[:, 0, :], masks[:, 1, :], [("qT", qi), "kb"])
                        p2, PT2 = score_exp(mm_c, 128, 0, sc, None, None, [("qT", qi), "kb"])
                        bkz = gemm_q(("zb", h))
                        zi, z = silu_z(bkz)
                        if j < 3:
                            hn = b_q(4 * (j + 1) + hq)
                        for sb in range(2):
                            cs2 = slice(sb * 128, (sb + 1) * 128)
                            cs2h = slice(256 + sb * 128, 256 + (sb + 1) * 128)
                            seq = [(vbw[:, sb, :], val(43 + sb, 128), PT1[:, cs2], ("PT", p1)),
                                   (vbw[:, sb + 1, :], val(44 + sb, 128), PT2[:, cs2], ("PT", p2)),
                                   (vbw[:, sb + 2, :], val(45 + sb, 128), PT1[:, cs2h], ("PT", p1))]
                            for n_, (vv, vl, pp, pk) in enumerate(seq):
                                P.op("pe", lambda e, vv=vv, pp=pp, cs2=cs2, n_=n_: e.matmul(
                                    A0b[:, cs2], vv, pp, start=(n_ == 0), stop=(n_ == 2)),
                                    reads=[pk, "vb"], writes=[("ps", BK_ACC[0])])
                                P.op("pe", lambda e, vl=vl, pp=pp, cs2=cs2, n_=n_: e.matmul(
                                    A1b[:, cs2], vl, pp, start=(n_ == 0), stop=(n_ == 2)),
                                    reads=[pk, "valw"], writes=[("ps", BK_ACC[1])])
                        if j < 3:
                            qnext.append(q_rope_finish(hn))
                        finish_head([(A0b[:, 0:256], A1b[:, 0:256], None)], esink[:, h:h + 1], z, zi,
                                    bzT[:, 8 + h, :], R(KC + 8 + h))
                    if j < 3:
                        b_win(j + 1)
                        qcur = qnext

                for h in range(4):
                    mq = []
                    for c in range(2):
                        bk = gemm_q(("mq", h, c))
                        mq.append(evac_q(bk))
                    mms = []
                    for kc2 in range(2):
                        mms.append((kc2, 0, 256, 128,
                                    [KmT[:, h * 2 + c, kc2 * 128:(kc2 + 1) * 128] for c in range(2)],
                                    [mq[c][1][:, :] for c in range(2)]))
                    pi, PT = score_exp(mms, 128, 128, scm, None, None, [("qT", mq[0][0]), ("qT", mq[1][0]), "KmT"])
                    zz = []
                    for c in range(2):
                        bk = gemm_q(("zm", h, c))
                        zz.append(silu_z(bk))
                    for kc2 in range(2):
                        P.op("pe", lambda e, kc2=kc2, PT=PT: e.matmul(A1b[:, 0:256], ones, PT[:, kc2 * 256:(kc2 + 1) * 256],
                                                                        start=(kc2 == 0), stop=(kc2 == 1)),
                             reads=[("PT", pi), "cst"], writes=[("ps", BK_ACC[1])])
                    for c in range(2):
                        for kc2 in range(2):
                            P.op("pe", lambda e, kc2=kc2, c=c, PT=PT, h=h: e.matmul(
                                A0b[:, c * 256:(c + 1) * 256], Vm[:, kc2, h * 256 + c * 128:h * 256 + (c + 1) * 128],
                                PT[:, kc2 * 256:(kc2 + 1) * 256], start=(kc2 == 0), stop=(kc2 == 1)),
                                reads=[("PT", pi), "Vm"], writes=[("ps", BK_ACC[0])])
                    for c in range(2):
                        zi, z = zz[c]
                        finish_head([(A0b[:, c * 256:(c + 1) * 256], A1b[:, 0:256], None)], None, z, zi,
                                    bzT[:, 24 + 2 * h + c, :], R(KC + 24 + 2 * h + c))

                bzk = [R(KC + i) for i in range(MIXC)]
                br_rows = [(0, 8), (8, 24), (24, 32)]
                for dc in range(KC):
                    for bi, nm in enumerate(("ga", "gb", "gm")):
                        bk = gemm_q((nm, dc))
                        P.op("act", lambda e, bk=bk, bi=bi: e.activation(out=gs[bi][:], in_=PS[bk][:, 0:TQ],
                                                                         func=AF.Sigmoid),
                             reads=[("ps", bk)], writes=[("gs", bi)])
                    wb, wbk = get_w("wbr", dc)
                    for bi in range(3):
                        gsb = gs[bi]
                        bk2 = gbank()
                        r0, r1 = br_rows[bi]
                        for rc in range(r0, r1):
                            P.op("pe", lambda e, bk2=bk2, rc=rc, wb=wb, r0=r0, r1=r1: e.matmul(
                                PS[bk2][:, 0:TQ], wb[:, rc, :], bzT[:, rc, :],
                                start=(rc == r0), stop=(rc == r1 - 1)),
                                reads=[wbk, bzk[rc]], writes=[("ps", bk2)])
                        if bi == 0:
                            P.op("dve", lambda e, bk2=bk2, gsb=gsb: e.tensor_tensor(
                                out=uacc[:], in0=PS[bk2][:, 0:TQ], in1=gsb[:], op=ALU.mult),
                                reads=[("ps", bk2), ("gs", bi)], writes=["uacc"])
                        else:
                            P.op("dve", lambda e, bk2=bk2, gsb=gsb: e.tensor_tensor(
                                out=utmp[:], in0=PS[bk2][:, 0:TQ], in1=gsb[:], op=ALU.mult),
                                reads=[("ps", bk2), ("gs", bi)], writes=["utmp"])
                            if bi == 1:
                                P.op("dve", lambda e: e.tensor_tensor(out=uacc[:], in0=uacc[:], in1=utmp[:], op=ALU.add),
                                     reads=["uacc", "utmp"], writes=["uacc"])
                            else:
                                P.op("dve", lambda e, dc=dc: e.tensor_tensor(
                                    out=uT[:, dc, :], in0=uacc[:], in1=utmp[:], op=ALU.add),
                                    reads=["uacc", "utmp"], writes=[R(KC + MIXC + dc)])

                uk = [R(KC + MIXC + i) for i in range(KC)]
                for sub in range(2):
                    P.dma("sp", lambda e, sub=sub, tok0=tok0: [
                        e.dma_start(out=xo[sub][:], in_=X[seg][MAXH + tok0 + sub * 128:MAXH + tok0 + (sub + 1) * 128, :])],
                        ("xo", sub), writes=[("xo", sub)])
                load_g("gf")
                for ob in range(KC // 2):
                    w4, wks = get_w_pair("wout", 2 * ob)
                    for sub in range(2):
                        bk = gbank()
                        pv = PS[bk][:, 0:256].rearrange("p (s c) -> p s c", s=2)
                        for kc in range(KC):
                            P.op("pe", lambda e, pv=pv, kc=kc, sub=sub, w4=w4: e.matmul(
                                pv, uT[:, kc, sub * 128:(sub + 1) * 128], w4[:, :, kc, :],
                                start=(kc == 0), stop=(kc == KC - 1)), reads=wks + [uk[kc]], writes=[("ps", bk)])
                        P.op("dve", lambda e, bk=bk, sub=sub, ob=ob: e.tensor_tensor(
                            out=xo[sub][:, ob * 256:(ob + 1) * 256], in0=PS[bk][:, 0:256],
                            in1=xo[sub][:, ob * 256:(ob + 1) * 256], op=ALU.add),
                            reads=[("ps", bk), ("xo", sub)], writes=[("xo", sub)])
                for sub in range(2):
                    xb = xo[sub]
                    c0 = 4 * sub
                    hbj = hbs[sub]
                    sk2 = ("stat2", sub)
                    P.op("act", lambda e, xb=xb, c0=c0, hbj=hbj: e.activation(out=hbj[:], in_=xb[:], func=AF.Square,
                                                                              accum_out=stat2[:, c0:c0 + 1]),
                         reads=[("xo", sub)], writes=[("hb", sub), sk2])
                    P.op("dve", lambda e, c0=c0: e.tensor_scalar(out=stat2[:, c0 + 1:c0 + 2], in0=stat2[:, c0:c0 + 1],
                                                                 scalar1=1.0 / D, scalar2=1e-6, op0=ALU.mult, op1=ALU.add),
                         reads=[sk2], writes=[sk2])
                    P.op("act", lambda e, c0=c0: e.activation(out=stat2[:, c0 + 2:c0 + 3], in_=stat2[:, c0 + 1:c0 + 2],
                                                              func=AF.Sqrt), reads=[sk2], writes=[sk2])
                    P.op("dve", lambda e, c0=c0: e.reciprocal(out=stat2[:, c0 + 3:c0 + 4], in_=stat2[:, c0 + 2:c0 + 3]),
                         reads=[sk2], writes=[sk2])
                    P.op("dve", lambda e, xb=xb, c0=c0: e.scalar_tensor_tensor(
                        out=xb[:], in0=xb[:], scalar=stat2[:, c0 + 3:c0 + 4], in1=gt[:], op0=ALU.mult, op1=ALU.mult),
                        reads=[("xo", sub), sk2, "gt"], writes=[("xo", sub)])
                    P.dma("sp", lambda e, xb=xb, sub=sub, tok0=tok0: [
                        e.dma_start(out=Y[seg][tok0 + sub * 128:tok0 + (sub + 1) * 128, :], in_=xb[:])],
                        ("xo", sub), reads=[("xo", sub)], writes=[("y", seg, t, sub)], final=True)

        def whole():
            wstate["n"] = 0
            for k in cnt:
                cnt[k] = 0
            for k in rr:
                rr[k] = 0
            load_consts()
            stage = 0
            for seg in "PS":
                stage += 1
                if stage > DEBUG_STOP:
                    return
                phase_m(seg)
                for grp in ("n", "d4", "d16"):
                    stage += 1
                    if stage > DEBUG_STOP:
                        return
                    phase_kv(seg, grp)
                stage += 1
                if stage > DEBUG_STOP:
                    return
                phase_q(seg)

        P.dry = True
        whole()
        P.dry = False
        cw_keys.clear()
        whole()
        P.emit()
        info = dict(n_ops=P.n_ops, n_wait=P.n_wait, sbuf_left=nc.sbuf_bytes_remaining, nplan=len(plan))
    return nc, info


def rope_tables(pos):
    inv = (10000.0 ** (-np.arange(0, HD, 2, dtype=np.float32) / HD)).astype(np.float32)
    ang = pos.astype(np.float32)[None, :] * inv[:, None]
    c = np.cos(ang).astype(np.float32)
    s = np.sin(ang).astype(np.float32)
    return np.concatenate([c, c], 0), np.concatenate([s, s], 0)


def const_table():
    ident = np.eye(128, dtype=np.float32)
    rotm = np.zeros((128, 128), np.float32)
    for m in range(64):
        rotm[m + 64, m] = -1.0
        rotm[m, m + 64] = 1.0
    ones = np.ones((128, 128), np.float32)
    kl = np.arange(128)[:, None]
    msk = []
    for nq, U in ((128, 2), (64, 4), (16, 16)):
        ql = np.arange(nq)[None, :]
        lo = (kl >= ql).astype(np.float32)
        hi = ((kl <= ql) & (kl < nq)).astype(np.float32)
        msk.append(np.tile(lo, (1, U)))
        msk.append(np.tile(hi, (1, U)))
    return np.concatenate([ident, rotm, ones] + msk, axis=1).astype(np.float32)


def block_layout(w, cols, KC):
    sub = w[:, cols]
    return np.ascontiguousarray(sub.reshape(KC, 128, len(cols)).transpose(1, 0, 2)).reshape(128, KC * len(cols))


def host_prepare(inputs, D):
    KC = D // 128
    NDB = D // 256
    w_in = np.asarray(inputs["w_in"][0], np.float32)
    w_mem = np.asarray(inputs["w_mem_kv"][0], np.float32)
    w_br = np.asarray(inputs["w_branch"][0], np.float32)
    w_out = np.asarray(inputs["w_out"][0], np.float32)
    blocks = in_col_blocks(D)
    win = np.stack([block_layout(w_in, c, KC) for c in blocks.values()])
    wmem = np.stack([block_layout(w_mem, np.arange(b * 128, (b + 1) * 128), KC) for b in range(16)])
    wbr = np.stack([block_layout(w_br, np.arange(b * 128, (b + 1) * 128), MIXC) for b in range(KC)])
    wout = np.stack([block_layout(w_out, np.arange(b * 128, (b + 1) * 128), KC) for b in range(KC)])
    cst = const_table()
    shared = dict(win=win, wmem=wmem, wbr=wbr, wout=wout, const=cst,
                  gn=np.asarray(inputs["g_norm"], np.float32).reshape(1, D),
                  gm=np.asarray(inputs["g_mem"], np.float32).reshape(1, D),
                  gf=np.asarray(inputs["g_final"], np.float32).reshape(1, D),
                  sink=np.asarray(inputs["attn_sink"], np.float32).reshape(1, 16))
    xs = {"P": np.asarray(inputs["x_prompt"], np.float32), "S": np.asarray(inputs["x_sample"], np.float32)}
    mems = {"P": np.asarray(inputs["mem_prompt"], np.float32), "S": np.asarray(inputs["mem_sample"], np.float32)}
    in_maps = []
    for c in range(NCORES):
        m = dict(shared)
        b = c // 4
        ch = c % 4
        for s in "PS":
            core = SEG_CORE[s]
            Ls = SEG_LEN[s]
            a = ch * core
            xe = np.zeros((core + 2 * MAXH + 16, D), np.float32)
            lo = max(0, a - MAXH)
            hi = min(Ls, a + core + MAXH)
            xe[lo - (a - MAXH):hi - (a - MAXH)] = xs[s][b, lo:hi]
            m["x" + s] = xe
            m["mem" + s] = np.ascontiguousarray(mems[s][b])
            cq, sq = rope_tables(np.arange(a, a + core))
            m["cosq" + s] = cq
            m["sinq" + s] = sq
            for g, (d, H) in GRP.items():
                L = core + 2 * H
                npc = L // d
                idx = np.arange(L)
                r, j = idx // npc, idx % npc
                pos = a - H + r + d * j
                ck, sk = rope_tables(np.clip(pos, 0, Ls - 1))
                m["cosk%s%s" % (s, g)] = ck
                m["sink%s%s" % (s, g)] = sk
            nt = core // TQ
            vw = np.zeros((nt, 128, NVAL), np.float32)
            p = np.arange(128)
            for t in range(nt):
                t0 = a + t * TQ

                def ok(pos):
                    return ((pos >= 0) & (pos < Ls)).astype(np.float32)

                for cc in range(3):
                    vw[t, :, cc] = ok(t0 - 64 + cc * 128 + p)
                for u in range(4):
                    vw[t, :, 3 + u] = ok(t0 + u + 4 * (p - 64))
                    vw[t, :, 7 + u] = ok(t0 + u + 4 * (p + 64))
                for u in range(16):
                    vw[t, :, 11 + u] = ok(t0 + u + 16 * (p - 64))
                    vw[t, :, 27 + u] = ok(t0 + u + 16 * (p + 64))
                for cc in range(4):
                    vw[t, :, 43 + cc] = ok(t0 - 128 + cc * 128 + p)
            m["valw" + s] = vw
        in_maps.append(m)
    return in_maps


_CACHE = {}


def run(inputs, D, trace=False):
    if D not in _CACHE:
        _CACHE[D] = build_program(D)
    nc, info = _CACHE[D]
    in_maps = host_prepare(inputs, D)
    res = run_bass_kernel_spmd(nc, in_maps, core_ids=list(range(NCORES)))
    B = 2
    yp = np.zeros((B, SEG_LEN["P"], D), np.float32)
    ys = np.zeros((B, SEG_LEN["S"], D), np.float32)
    for c in range(NCORES):
        b, ch = c // 4, c % 4
        r = res.results[c]
        yp[b, ch * 1024:(ch + 1) * 1024] = r["yP"]
        ys[b, ch * 2048:(ch + 1) * 2048] = r["yS"]
    return yp, ys


def kernel(x_prompt, x_sample, mem_prompt, mem_sample, g_norm, w_in, attn_sink, g_mem, w_mem_kv, w_branch, w_out,
           g_final):
    D = int(np.asarray(x_prompt).shape[-1])
    inputs = dict(x_prompt=x_prompt, x_sample=x_sample, mem_prompt=mem_prompt, mem_sample=mem_sample, g_norm=g_norm,
                  w_in=w_in, attn_sink=attn_sink, g_mem=g_mem, w_mem_kv=w_mem_kv, w_branch=w_branch, w_out=w_out,
                  g_final=g_final)
    return run(inputs, D)
```

```python
import numpy as np
import concourse.bass as bass
import concourse.mybir as mybir
from concourse.bass_utils import run_bass_kernel_spmd
from contextlib import ExitStack

F32 = mybir.dt.float32
BF16 = mybir.dt.bfloat16
AF = mybir.ActivationFunctionType
ALU = mybir.AluOpType
ENGS = ("pe", "act", "dve", "pool", "sp")


class Op:
    __slots__ = ("eng", "fn", "deps", "is_dma", "sig", "token", "n_dma", "idx")

    def __init__(self, eng, fn, is_dma=False, n_dma=1):
        self.eng = eng
        self.fn = fn
        self.deps = ()
        self.is_dma = is_dma
        self.sig = False
        self.token = None
        self.n_dma = n_dma
        self.idx = -1


class Prog:
    def __init__(self, nc, stack):
        self.nc = nc
        self.stack = stack
        self.ops = {e: [] for e in ENGS}
        self.last_w = {}
        self.readers = {}
        self.dry = False
        self.eng_sem = {e: stack.enter_context(nc.semaphore("s_" + e)) for e in ENGS}
        self.dma_sems = {}
        self.dma_cnt = {}
        self.final_tokens = []
        self.last_dma = {}

    def _track(self, o, reads, writes):
        deps = []
        seen = set()

        def add(d):
            if d is not None and d is not o and id(d) not in seen:
                seen.add(id(d))
                deps.append(d)

        for k in reads:
            add(self.last_w.get(k))
        for k in writes:
            add(self.last_w.get(k))
            for r in self.readers.get(k, ()):
                add(r)
        o.deps = deps
        for k in reads:
            self.readers.setdefault(k, []).append(o)
        for k in writes:
            self.last_w[k] = o
            self.readers[k] = []

    def op(self, eng, fn, reads=(), writes=()):
        if self.dry:
            return None
        ps_r = [k for k in reads if isinstance(k, tuple) and k[0] == "ps"]
        if ps_r:
            reads = [k for k in reads if not (isinstance(k, tuple) and k[0] == "ps")]
            writes = list(writes) + ps_r
        o = Op(eng, fn)
        self._track(o, reads, writes)
        o.idx = len(self.ops[eng])
        self.ops[eng].append(o)
        return o

    def dma(self, eng, fn, semkey, reads=(), writes=(), n=1, final=False):
        if self.dry:
            return None
        o = Op(eng, fn, is_dma=True, n_dma=n)
        if semkey not in self.dma_sems:
            self.dma_sems[semkey] = self.stack.enter_context(self.nc.semaphore("d_%d" % len(self.dma_sems)))
            self.dma_cnt[semkey] = 0
        self.dma_cnt[semkey] += 16 * n
        o.token = (self.dma_sems[semkey], self.dma_cnt[semkey])
        self._track(o, reads, writes)
        prev = self.last_dma.get(semkey)
        if prev is not None and all(prev is not d for d in o.deps):
            o.deps.append(prev)
        self.last_dma[semkey] = o
        o.idx = len(self.ops[eng])
        self.ops[eng].append(o)
        if final:
            self.final_tokens.append(o.token)
        return o

    @staticmethod
    def _need_wait(o, d):
        if d.is_dma:
            return True
        if d.eng == o.eng and o.eng == "pe" and not o.is_dma:
            return False
        return True

    def _reduce(self, o):
        comp = {}
        dmas = {}
        for d in o.deps:
            if not self._need_wait(o, d):
                continue
            if d.is_dma:
                sem, val = d.token
                if sem.num not in dmas or dmas[sem.num][1] < val:
                    dmas[sem.num] = (sem, val)
            else:
                if d.eng not in comp or comp[d.eng].idx < d.idx:
                    comp[d.eng] = d
        return comp, dmas

    def emit(self):
        nc = self.nc
        for e in ENGS:
            for o in self.ops[e]:
                comp, dmas = self._reduce(o)
                o.deps = (comp, dmas)
                for d in comp.values():
                    d.sig = True
        for e in ENGS:
            c = 0
            for o in self.ops[e]:
                if not o.is_dma and o.sig:
                    c += 1
                    o.token = (self.eng_sem[e], c)
        self.n_wait = {e: 0 for e in ENGS}
        self.n_ops = {e: len(self.ops[e]) for e in ENGS}

        def run(ename, eng):
            seen = {}

            def wait(sem, val):
                if seen.get(sem.num, 0) >= val:
                    return
                seen[sem.num] = val
                eng.wait_ge(sem, val)
                self.n_wait[ename] += 1

            for o in self.ops[ename]:
                comp, dmas = o.deps
                for d in comp.values():
                    wait(*d.token)
                for sem, val in dmas.values():
                    wait(sem, val)
                r = o.fn(eng)
                if o.is_dma:
                    if not isinstance(r, (list, tuple)):
                        r = [r]
                    assert len(r) == o.n_dma, (len(r), o.n_dma)
                    for ins in r:
                        ins.then_inc(o.token[0], 16)
                elif o.sig:
                    r.then_inc(o.token[0], 1)
            if ename == "sp":
                for sem, val in self.final_tokens:
                    wait(sem, val)

        with nc.Block() as block:
            @block.tensor
            def _(eng):
                run("pe", eng)

            @block.scalar
            def _(eng):
                run("act", eng)

            @block.vector
            def _(eng):
                run("dve", eng)

            @block.gpsimd
            def _(eng):
                run("pool", eng)

            @block.sync
            def _(eng):
                run("sp", eng)


NCORES = 8
HD = 128
MAXH = 1024
TQ = 256
SEG_CORE = {"P": 1024, "S": 2048}
SEG_LEN = {"P": 4096, "S": 8192}
GRP = {"n": (1, 128), "d4": (4, 256), "d16": (16, 1024)}
MIXC = 32
NVAL = 47
DEBUG_STOP = 100
DEBUG_QT = 100
DEBUG_KV = 100
DEBUG_ROPE = 100
DEBUG_V1 = 0


def in_col_blocks(D):
    o_bq = 9216
    o_bkv = o_bq + 2048
    o_mq = o_bkv + 1024
    o_za = o_mq + 1024
    o_zb = o_za + 1024
    o_zm = o_zb + 2048
    o_ga = o_zm + 1024
    o_gb = o_ga + D
    o_gm = o_gb + D
    ar = np.arange(128)

    def aq(t, g, h):
        return ((t * 3 + g) * 8 + h) * 128 + ar

    blocks = {}
    for g in range(3):
        for h in range(8):
            blocks[("ak", g, h)] = aq(1, g, h)
        for h in range(8):
            blocks[("av", g, h)] = aq(2, g, h)
    for h in range(4):
        blocks[("bk", h)] = o_bkv + h * 128 + ar
    for h in range(4):
        blocks[("bv", h)] = o_bkv + (4 + h) * 128 + ar
    for s in range(8):
        for g in range(3):
            blocks[("qa", s, g)] = aq(0, g, s)
        blocks[("za", s)] = o_za + s * 128 + ar
    for h in range(16):
        blocks[("qb", h)] = o_bq + h * 128 + ar
        blocks[("zb", h)] = o_zb + h * 128 + ar
    for h in range(4):
        for c in range(2):
            blocks[("mq", h, c)] = o_mq + h * 256 + c * 128 + ar
            blocks[("zm", h, c)] = o_zm + h * 256 + c * 128 + ar
    for dc in range(D // 128):
        blocks[("ga", dc)] = o_ga + dc * 128 + ar
        blocks[("gb", dc)] = o_gb + dc * 128 + ar
        blocks[("gm", dc)] = o_gm + dc * 128 + ar
    return blocks


def kv_tiles(seg, grp):
    d, H = GRP[grp]
    L = SEG_CORE[seg] + 2 * H
    out = []
    i = 0
    while i < L:
        n = min(512, L - i)
        out.append((i, n))
        i += n
    return d, H, L, L // d, out


def build_program(D):
    KC = D // 128
    WS = max(KC, MIXC) * 128
    G4 = min(4, KC)
    blocks = in_col_blocks(D)
    bnames = list(blocks.keys())
    bidx = {n: i for i, n in enumerate(bnames)}
    NBIN = len(bnames)

    nc = bass.Bass("TRN2", target_bir_lowering=False)
    dt_in = lambda name, shape: nc.dram_tensor(name, list(shape), F32, kind="ExternalInput").ap()
    X = {s: dt_in("x" + s, [SEG_CORE[s] + 2 * MAXH + 16, D]) for s in "PS"}
    MEM = {s: dt_in("mem" + s, [256, D]) for s in "PS"}
    GN = dt_in("gn", [1, D])
    GM = dt_in("gm", [1, D])
    GF = dt_in("gf", [1, D])
    SINK = dt_in("sink", [1, 16])
    WIN = dt_in("win", [NBIN, 128, KC * 128])
    WMEM = dt_in("wmem", [16, 128, KC * 128])
    WBR = dt_in("wbr", [KC, 128, MIXC * 128])
    WOUT = dt_in("wout", [KC, 128, KC * 128])
    COSQ = {s: dt_in("cosq" + s, [128, SEG_CORE[s]]) for s in "PS"}
    SINQ = {s: dt_in("sinq" + s, [128, SEG_CORE[s]]) for s in "PS"}
    COSK = {}
    SINK_ = {}
    for s in "PS":
        for g in GRP:
            L = SEG_CORE[s] + 2 * GRP[g][1]
            COSK[s, g] = dt_in("cosk%s%s" % (s, g), [128, L])
            SINK_[s, g] = dt_in("sink%s%s" % (s, g), [128, L])
    VALW = {s: dt_in("valw" + s, [SEG_CORE[s] // TQ, 128, NVAL]) for s in "PS"}
    MASKW = {s: dt_in("maskw" + s, [SEG_CORE[s] // TQ, 128, 6 * 256]) for s in "PS"}
    CONST = dt_in("const", [128, 3 * 128 + 2 * 256])
    Y = {s: nc.dram_tensor("y" + s, [SEG_CORE[s], D], F32, kind="ExternalOutput").ap() for s in "PS"}

    dint = lambda name, shape: nc.dram_tensor(name, list(shape), BF16, kind="Internal").ap()
    WSCR = {"win": dint("s_win", [NBIN, 128, KC * 128]), "wmem": dint("s_wmem", [16, 128, KC * 128]),
            "wbr": dint("s_wbr", [KC, 128, MIXC * 128]), "wout": dint("s_wout", [KC, 128, KC * 128])}
    WSRC = {"win": WIN, "wmem": WMEM, "wbr": WBR, "wout": WOUT}
    KTC = {}
    VC = {}
    for s in "PS":
        for gi, g in enumerate(("n", "d4", "d16")):
            L = SEG_CORE[s] + 2 * GRP[g][1]
            KTC[s, "a", gi] = dint("kt_%s_a%d" % (s, gi), [8, 128, L])
            VC[s, "a", gi] = dint("v_%s_a%d" % (s, gi), [L, 1024])
        L = SEG_CORE[s] + 2 * GRP["n"][1]
        KTC[s, "b", 0] = dint("kt_%s_b" % s, [4, 128, L])
        VC[s, "b", 0] = dint("v_%s_b" % s, [L, 512])

    st = ExitStack()
    with st:
        P = Prog(nc, st)
        sbt = lambda name, shape, dt: st.enter_context(nc.sbuf_tensor(name, list(shape), dt))
        PS = [st.enter_context(nc.psum_tensor("psb%d" % i, [128, 512], F32)) for i in range(8)]
        BK_G = (0, 1)
        BK_G4 = (0, 1, 3, 4)
        BK_ROT = 2
        BK_S = (3, 4)
        BK_ACC = (5, 6, 7)
        cnt = {"g": 0, "s": 0, "g4": 0}

        def gbank():
            b = BK_G[cnt["g"] % 2]
            cnt["g"] += 1
            return b

        def gbank4():
            b = BK_G4[cnt["g4"] % 4]
            cnt["g4"] += 1
            return b

        def sbank():
            b = BK_S[cnt["s"] % 2]
            cnt["s"] += 1
            return b

        ACTB = sbt("actb", [128, KC * 256 + MIXC * 256 + KC * 256], BF16)
        hT = ACTB[:, 0:KC * 256].rearrange("p (k n) -> p k n", k=KC)
        bzT = ACTB[:, KC * 256:KC * 256 + MIXC * 256].rearrange("p (k n) -> p k n", k=MIXC)
        uT = ACTB[:, KC * 256 + MIXC * 256:].rearrange("p (k n) -> p k n", k=KC)
        hTkv = ACTB[:, 0:KC * 512].rearrange("p (k n) -> p k n", k=KC)
        xo = [sbt("xo%d" % i, [128, D], F32) for i in range(2)]
        hbs = [sbt("hb%d" % i, [128, D], BF16) for i in range(2)]
        stat2 = sbt("stat2", [128, 8], F32)
        fcnt = {"n": 0}
        gt = sbt("gt", [128, D], F32)
        NSLOT = 4
        WSL = sbt("wsl", [128, NSLOT, WS], BF16)
        cst = sbt("cst", [128, 3 * 128 + 2 * 256], BF16)
        maskt = sbt("maskt", [128, 6, 256], BF16)
        ident = cst[:, 0:128]
        rotm = cst[:, 128:256]
        ones = cst[:, 256:384]
        masks = cst[:, 384:].rearrange("p (m n) -> p m n", m=2)
        sinkt = sbt("sinkt", [128, 16], F32)
        esink = sbt("esink", [128, 16], F32)
        stat = sbt("stat", [128, 16], F32)
        cosb = sbt("cosb", [128, 512], F32)
        sinb = sbt("sinb", [128, 512], F32)
        qbf = [sbt("qbf%d" % i, [128, 512], BF16) for i in range(2)]
        t1b = [sbt("t1b%d" % i, [128, 512], F32) for i in range(1)]
        t2b = [sbt("t2b%d" % i, [128, 512], F32) for i in range(1)]
        kout = [sbt("kout%d" % i, [128, 512], BF16) for i in range(2)]
        vout = [sbt("vout%d" % i, [128, 256], BF16) for i in range(2)]
        qT = [sbt("qT%d" % i, [128, 256], BF16) for i in range(8)]
        zs = [sbt("zs%d" % i, [128, 256], F32) for i in range(2)]
        PTb = [sbt("PTb%d" % i, [128, 512], BF16) for i in range(5)]
        rden = sbt("rden", [128, 256], F32)
        tnum = sbt("tnum", [128, 256], F32)
        tsum = sbt("tsum", [128, 256], F32)
        gs = [sbt("gs%d" % i, [128, 256], F32) for i in range(3)]
        uacc = tsum
        utmp = tnum
        valw = sbt("valw", [128, NVAL], BF16)
        KmT = sbt("KmT", [128, 8, 256], BF16)
        Vm = sbt("Vm", [128, 2, 1024], BF16)
        k1 = sbt("k1", [128, 384], BF16)
        k2 = sbt("k2", [128, 4, 192], BF16)
        k3 = sbt("k3", [128, 16, 144], BF16)
        v1 = sbt("v1", [128, 3, 128], BF16)
        v2lo = sbt("v2lo", [128, 4, 128], BF16)
        v2hi = sbt("v2hi", [128, 4, 128], BF16)
        v3lo = sbt("v3lo", [128, 16, 128], BF16)
        v3hi = sbt("v3hi", [128, 16, 128], BF16)
        kbw = sbt("kbw", [128, 512], BF16)
        vbw = sbt("vbw", [128, 4, 128], BF16)
        rr = {"q": 0, "t": 0, "k": 0, "v": 0, "qT": 0, "z": 0, "PT": 0}

        def rot(name, lst):
            i = rr[name] % len(lst)
            rr[name] += 1
            return i, lst[i]

        plan = []
        wstate = {"n": 0, "issued": 0, "cast": set(), "ncast": 0, "castptr": 0, "qstart": 0}

        sched = []

        def build_sched():
            fu = {}
            for i, e in enumerate(plan):
                if e is not None and e not in fu:
                    fu[e] = i
            qs_ = wstate["qstart"]
            early = sorted([(i, e) for e, i in fu.items() if i < qs_])
            late = sorted([(i, e) for e, i in fu.items() if i >= qs_])
            for i, e in early:
                sched.append((i - 16, e[0], e[1]))
            nq = max(1, len(late))
            for r_, (i, e) in enumerate(late):
                dep = int(16 + r_ * max(1, qs_ - 64) / nq)
                sched.append((min(dep, i - 16), e[0], e[1]))
            sched.sort(key=lambda t: t[0])

        def w_issue(m):
            cp = wstate["castptr"]
            while cp < len(sched) and sched[cp][0] < m:
                dep, f2, b2 = sched[cp]
                while dep >= 0 and plan[dep] is None:
                    dep -= 1
                wstate["cast"].add((f2, b2))
                P.dma("pool", lambda e, f2=f2, b2=b2: [e.dma_start(out=WSCR[f2][b2], in_=WSRC[f2][b2])],
                      ("wcast", wstate["ncast"] % 8), reads=([("wl", dep)] if dep >= 0 else []),
                      writes=[("wscr", f2, b2)])
                wstate["ncast"] += 1
                cp += 1
            wstate["castptr"] = cp
            if plan[m] is None:
                return
            fam, blk = plan[m]
            assert (fam, blk) in wstate["cast"]
            slot = m % NSLOT
            n_el = (MIXC if fam == "wbr" else KC) * 128
            P.dma("sp", lambda e, fam=fam, blk=blk, slot=slot, n_el=n_el:
                  [e.dma_start(out=WSL[:, slot, 0:n_el], in_=WSCR[fam][blk])],
                  ("w", slot), reads=[("wscr", fam, blk)], writes=[("w", slot), ("wl", m)])

        def _req(entry):
            n = wstate["n"]
            wstate["n"] += 1
            if P.dry:
                plan.append(entry)
            else:
                assert plan[n] == entry, (n, plan[n], entry)
                while wstate["issued"] < min(n + NSLOT - 1, len(plan)):
                    w_issue(wstate["issued"])
                    wstate["issued"] += 1
                if wstate["issued"] <= n:
                    w_issue(n)
                    wstate["issued"] = n + 1
            return n % NSLOT

        def get_w(fam, blk):
            slot = _req((fam, blk))
            kk = MIXC if fam == "wbr" else KC
            return WSL[:, slot, 0:kk * 128].rearrange("p (k c) -> p k c", k=kk), ("w", slot)

        def get_w_pair(fam, blk):
            if wstate["n"] % 2 == 1:
                _req(None)
            s0 = _req((fam, blk))
            s1 = _req((fam, blk + 1))
            assert s1 == s0 + 1
            return (WSL[:, s0:s0 + 2, 0:KC * 128].rearrange("p s (k c) -> p s k c", k=KC),
                    [("w", s0), ("w", s1)])

        def load_consts():
            P.dma("pool", lambda e: [e.dma_start(out=cst[:], in_=CONST)], "cst", writes=["cst"])
            P.dma("sp", lambda e: [e.dma_start(out=sinkt[:], in_=SINK.partition_broadcast(128))], "sinkt",
                  writes=["sinkt"])
            P.op("act", lambda e: e.activation(out=esink[:], in_=sinkt[:], func=AF.Exp), reads=["sinkt"],
                 writes=["esink"])

        def load_g(which):
            src = {"gn": GN, "gm": GM, "gf": GF}[which]
            P.dma("sp", lambda e: [e.dma_start(out=gt[:], in_=src.partition_broadcast(128))], "gt", writes=["gt"])

        def front_a(xsrc, runs, b):
            xb = xo[b]
            par = fcnt["n"] % 2
            fcnt["n"] += 1
            hb = hbs[par]
            hk = ("hb", par)
            sk = ("stat", par)
            so = 4 * par

            def ld(e):
                r = []
                for (p0, cn, row0, rs) in runs:
                    if rs == 1:
                        src = xsrc[row0:row0 + cn, :]
                    else:
                        src = xsrc[row0:row0 + cn * rs, :].rearrange("(i s) d -> i s d", s=rs)[:, 0, :]
                    r.append(e.dma_start(out=xb[p0:p0 + cn, :], in_=src))
                return r

            P.dma("sp", ld, ("xo", b), writes=[("xo", b)], n=len(runs))
            P.op("act", lambda e: e.activation(out=hb[:], in_=xb[:], func=AF.Square, accum_out=stat[:, so:so + 1]),
                 reads=[("xo", b)], writes=[hk, sk])
            P.op("dve", lambda e: e.tensor_scalar(out=stat[:, so + 1:so + 2], in0=stat[:, so:so + 1], scalar1=1.0 / D,
                                                  scalar2=1e-6, op0=ALU.mult, op1=ALU.add), reads=[sk], writes=[sk])
            P.op("act", lambda e: e.activation(out=stat[:, so + 2:so + 3], in_=stat[:, so + 1:so + 2], func=AF.Sqrt),
                 reads=[sk], writes=[sk])
            P.op("dve", lambda e: e.reciprocal(out=stat[:, so + 3:so + 4], in_=stat[:, so + 2:so + 3]), reads=[sk],
                 writes=[sk])
            P.op("dve", lambda e: e.scalar_tensor_tensor(out=hb[:], in0=xb[:], scalar=stat[:, so + 3:so + 4], in1=gt[:],
                                                         op0=ALU.mult, op1=ALU.mult),
                 reads=[("xo", b), sk, "gt"], writes=[hk])
            return (hb, hk)

        def front_b(hnd, dst, dstkeyf, col0, bankf):
            hb, hk = hnd
            for q in range(KC // G4):
                bk = bankf()
                pt = PS[bk][:, :].bitcast(BF16)
                for i in range(G4):
                    kc = q * G4 + i
                    P.op("pe", lambda e, pt=pt, i=i, kc=kc: e.transpose(out=pt[:, i * 128:(i + 1) * 128],
                                                                       in_=hb[:, kc * 128:(kc + 1) * 128],
                                                                       identity=ident),
                         reads=[hk, "cst"], writes=[("ps", bk)])
                src = pt[:, 0:G4 * 128].rearrange("p (a b) -> p a b", a=G4)
                dv = dst[:, q * G4:(q + 1) * G4, col0:col0 + 128]
                if q % 2 == 0:
                    P.op("act", lambda e, src=src, dv=dv: e.activation(out=dv, in_=src, func=AF.Copy),
                         reads=[("ps", bk)], writes=dstkeyf(q))
                else:
                    P.op("dve", lambda e, src=src, dv=dv: e.tensor_copy(out=dv, in_=src),
                         reads=[("ps", bk)], writes=dstkeyf(q))

        def front_pipe(specs, dst, dstkeyf, bankf, pre=None, nxt=None):
            hs = list(pre) if pre else []
            n = len(specs)
            while len(hs) < min(2, n):
                xs_, runs, b, _ = specs[len(hs)]
                hs.append(front_a(xs_, runs, b))
            for i in range(n):
                front_b(hs[i], dst, dstkeyf, specs[i][3], bankf)
                if i + 2 < n:
                    xs_, runs, b, _ = specs[i + 2]
                    hs.append(front_a(xs_, runs, b))
            out = []
            if nxt:
                for j in range(min(2, len(nxt))):
                    xs_, runs, b, _ = nxt[j]
                    out.append(front_a(xs_, runs, b))
            return out

        R = lambda j: ("R", j)
        hq_keys = lambda q: [R(j) for j in range(q * G4, (q + 1) * G4)]
        kvq_keys = lambda q: [R(j) for j in range(2 * q * G4, 2 * (q + 1) * G4)]
        h_kc = lambda kc: [R(kc)]
        kv_kc = lambda kc: [R(2 * kc), R(2 * kc + 1)]

        def gemm_fm(w, wkey, src3, srckeys, ntok, bk):
            pt = PS[bk]
            for kc in range(KC):
                P.op("pe", lambda e, kc=kc: e.matmul(pt[:, 0:ntok], w[:, kc, :], src3[:, kc, 0:ntok],
                                                     start=(kc == 0), stop=(kc == KC - 1)),
                     reads=[wkey] + srckeys(kc), writes=[("ps", bk)])
            return pt

        def gemm_q(name):
            w, wk = get_w("win", bidx[name])
            bk = gbank()
            gemm_fm(w, wk, hT, h_kc, TQ, bk)
            return bk

        def rope_a(bk, n):
            qi, qb_ = rot("q", qbf)
            P.op("act", lambda e: e.activation(out=qb_[:, 0:n], in_=PS[bk][:, 0:n], func=AF.Copy),
                 reads=[("ps", bk)], writes=[("qbf", qi)])
            return (bk, n, qi, qb_)

        def rope_b(hnd, dest, destkeys):
            bk, n, qi, qb_ = hnd
            pt = PS[bk]
            ti, t1 = 0, t1b[0]
            t2 = t2b[0]
            rp = PS[BK_ROT]
            P.op("pe", lambda e: e.matmul(rp[:, 0:n], rotm, qb_[:, 0:n], start=True, stop=True),
                 reads=[("qbf", qi), "cst"], writes=[("ps", BK_ROT)])
            P.op("dve", lambda e: e.tensor_tensor(out=t1[:, 0:n], in0=pt[:, 0:n], in1=cosb[:, 0:n], op=ALU.mult),
                 reads=[("ps", bk), "cos"], writes=[("t1", ti)])
            P.op("dve", lambda e: e.tensor_tensor(out=t2[:, 0:n], in0=rp[:, 0:n], in1=sinb[:, 0:n], op=ALU.mult),
                 reads=[("ps", BK_ROT), "sin"], writes=["t2"])
            P.op("dve", lambda e: e.tensor_tensor(out=dest, in0=t1[:, 0:n], in1=t2[:, 0:n], op=ALU.add),
                 reads=[("t1", ti), "t2"], writes=destkeys)

        def phase_m(seg):
            load_g("gm")
            front_pipe([(MEM[seg], [(0, 128, sub * 128, 1)], sub, sub * 128) for sub in range(2)], hT, hq_keys, gbank)
            for h in range(4):
                for c in range(2):
                    w, wk = get_w("wmem", h * 2 + c)
                    bk = gbank()
                    pt = gemm_fm(w, wk, hT, h_kc, 256, bk)
                    P.op("act", lambda e, pt=pt, h=h, c=c: e.activation(out=KmT[:, h * 2 + c, :], in_=pt[:, 0:256],
                                                                        func=AF.Copy),
                         reads=[("ps", bk)], writes=["KmT"])
            for h in range(4):
                w4, wks = get_w_pair("wmem", 8 + 2 * h)
                for sub in range(2):
                    bk = gbank()
                    pv = PS[bk][:, 0:256].rearrange("p (s c) -> p s c", s=2)
                    for kc in range(KC):
                        P.op("pe", lambda e, pv=pv, kc=kc, sub=sub, w4=w4: e.matmul(
                            pv, hT[:, kc, sub * 128:(sub + 1) * 128], w4[:, :, kc, :],
                            start=(kc == 0), stop=(kc == KC - 1)), reads=wks + h_kc(kc), writes=[("ps", bk)])
                    P.op("dve", lambda e, bk=bk, h=h, sub=sub: e.tensor_copy(out=Vm[:, sub, h * 256:(h + 1) * 256],
                                                                             in_=PS[bk][:, 0:256]),
                         reads=[("ps", bk)], writes=["Vm"])

        cw_keys = {}

        def phase_kv(seg, grp):
            d, H, L, npc, tiles = kv_tiles(seg, grp)
            base = MAXH - H
            gi = {"n": 0, "d4": 1, "d16": 2}[grp]
            kblocks = [(("ak", gi, h), KTC[seg, "a", gi], h) for h in range(8)]
            vblocks = [(("av", gi, 2 * hp), VC[seg, "a", gi], hp * 256) for hp in range(4)]
            if grp == "n":
                kblocks += [(("bk", h), KTC[seg, "b", 0], h) for h in range(4)]
                vblocks += [(("bv", 2 * hp), VC[seg, "b", 0], hp * 256) for hp in range(2)]
            keys = cw_keys.setdefault((seg, grp), [])
            load_g("gn")
            def tile_specs(i0, nt):
                sp_ = []
                for sub in range(nt // 128):
                    runs = []
                    idx = i0 + sub * 128
                    p0 = 0
                    while p0 < 128:
                        r, j = divmod(idx + p0, npc)
                        cn = min(128 - p0, npc - j)
                        runs.append((p0, cn, base + r + d * j, d))
                        p0 += cn
                    sp_.append((X[seg], runs, sub % 2, sub * 128))
                return sp_

            pre = None
            for ti_, (i0, nt) in enumerate(tiles):
                nsub = nt // 128
                nxt_specs = tile_specs(*tiles[ti_ + 1]) if ti_ + 1 < len(tiles) else None
                pre = front_pipe(tile_specs(i0, nt), hTkv, kvq_keys, gbank4, pre=pre, nxt=nxt_specs)
                P.dma("sp", lambda e, i0=i0, nt=nt: [e.dma_start(out=cosb[:, 0:nt], in_=COSK[seg, grp][:, i0:i0 + nt])],
                      "cos", writes=["cos"])
                P.dma("sp", lambda e, i0=i0, nt=nt: [e.dma_start(out=sinb[:, 0:nt], in_=SINK_[seg, grp][:, i0:i0 + nt])],
                      "sin", writes=["sin"])

                def k_finish(pend):
                    hnd, cache, h = pend
                    ki, ko = rot("k", kout)
                    rope_b(hnd, ko[:, 0:nt], [("kout", ki)])
                    key = ("cw", seg, grp, len(keys))
                    keys.append(key)
                    P.dma("sp", lambda e, ko=ko, cache=cache, h=h, i0=i0, nt=nt:
                          [e.dma_start(out=cache[h][:, i0:i0 + nt], in_=ko[:, 0:nt])],
                          ("kout", ki), reads=[("kout", ki)], writes=[key])

                pend = None
                for (bn, cache, h) in kblocks:
                    w, wk = get_w("win", bidx[bn])
                    bk = gbank4()
                    gemm_fm(w, wk, hTkv, kv_kc, nt, bk)
                    hnd = rope_a(bk, nt)
                    if pend is not None:
                        k_finish(pend)
                    pend = (hnd, cache, h)
                k_finish(pend)
                for (bn, cache, c0) in vblocks:
                    w4, wks = get_w_pair("win", bidx[bn])
                    for sub in range(nsub):
                        bk = gbank4()
                        pv = PS[bk][:, 0:256].rearrange("p (s c) -> p s c", s=2)
                        for kc in range(KC):
                            P.op("pe", lambda e, pv=pv, kc=kc, sub=sub, w4=w4: e.matmul(
                                pv, hTkv[:, kc, sub * 128:(sub + 1) * 128], w4[:, :, kc, :],
                                start=(kc == 0), stop=(kc == KC - 1)), reads=wks + kv_kc(kc), writes=[("ps", bk)])
                        vi, vo = rot("v", vout)
                        if sub % 2 == 0:
                            P.op("act", lambda e, bk=bk, vo=vo: e.activation(out=vo[:], in_=PS[bk][:, 0:256], func=AF.Copy),
                                 reads=[("ps", bk)], writes=[("vout", vi)])
                        else:
                            P.op("dve", lambda e, bk=bk, vo=vo: e.tensor_copy(out=vo[:], in_=PS[bk][:, 0:256]),
                                 reads=[("ps", bk)], writes=[("vout", vi)])
                        key = ("cw", seg, grp, len(keys))
                        keys.append(key)
                        r0 = i0 + sub * 128
                        P.dma("sp", lambda e, vo=vo, cache=cache, r0=r0, c0=c0:
                              [e.dma_start(out=cache[r0:r0 + 128, c0:c0 + 256], in_=vo[:])],
                              ("vout", vi), reads=[("vout", vi)], writes=[key])
            P.op("sp", lambda e: e.nop(), reads=list(keys), writes=[("cache", seg, grp)])

        def q_rope_finish(hnd):
            qi, qt = rot("qT", qT)
            rope_b(hnd, qt[:, :], [("qT", qi)])
            return qi, qt

        def evac_q(bk):
            qi, qt = rot("qT", qT)
            P.op("act", lambda e: e.activation(out=qt[:, :], in_=PS[bk][:, 0:TQ], func=AF.Copy),
                 reads=[("ps", bk)], writes=[("qT", qi)])
            return qi, qt

        def silu_z(bk):
            zi, z = rot("z", zs)
            P.op("act", lambda e: e.activation(out=z[:], in_=PS[bk][:, 0:TQ], func=AF.Silu),
                 reads=[("ps", bk)], writes=[("zs", zi)])
            return zi, z

        def score_exp(mms, rows_lo, rows_hi, scale, mask_lo, mask_hi, extra_reads):
            bk = sbank()
            pt = PS[bk]
            for (half, c0, ncol, nrow, lhs, rhs) in mms:
                for i, (l, r) in enumerate(zip(lhs, rhs)):
                    P.op("pe", lambda e, half=half, c0=c0, ncol=ncol, nrow=nrow, l=l, r=r, i=i, nl=len(lhs):
                         e.matmul(pt[0:nrow, half * 256 + c0:half * 256 + c0 + ncol], l, r, start=(i == 0),
                                  stop=(i == nl - 1)),
                         reads=extra_reads, writes=[("ps", bk)])
            pi, PT = rot("PT", PTb)
            for half, rows, mask in ((0, rows_lo, mask_lo), (1, rows_hi, mask_hi)):
                if rows == 0:
                    continue
                sl = slice(half * 256, half * 256 + 256)
                P.op("act", lambda e, rows=rows, sl=sl: e.activation(out=PT[0:rows, sl], in_=pt[0:rows, sl],
                                                                     func=AF.Exp, scale=scale),
                     reads=[("ps", bk)], writes=[("PT", pi)])
                if mask is not None:
                    P.op("dve", lambda e, rows=rows, sl=sl, mask=mask: e.tensor_tensor(
                        out=PT[0:rows, sl], in0=PT[0:rows, sl], in1=mask[0:rows, :], op=ALU.mult),
                        reads=["cst", "maskt"], writes=[("PT", pi)])
            return pi, PT

        def finish_head(parts, extra_den, z, zi, dest, destkey):
            bks = [("ps", b) for b in BK_ACC]

            def vw(ps_ap, sb_tile, U):
                if U is None:
                    return ps_ap, sb_tile[:]
                return (ps_ap.rearrange("p (u j) -> p j u", u=U), sb_tile[:].rearrange("p (j u) -> p j u", u=U))

            acc0, den0, _ = parts[0]
            if extra_den is not None:
                P.op("dve", lambda e: e.tensor_scalar_add(out=tsum[:], in0=den0, scalar1=extra_den),
                     reads=bks + ["esink"], writes=["tsum"])
            else:
                P.op("dve", lambda e: e.tensor_copy(out=tsum[:], in_=den0), reads=bks, writes=["tsum"])
            for (_, dn, U) in parts[1:]:
                pv, sv = vw(dn, tsum, U)
                P.op("dve", lambda e, pv=pv, sv=sv: e.tensor_tensor(out=sv, in0=pv, in1=sv, op=ALU.add),
                     reads=bks + ["tsum"], writes=["tsum"])
            P.op("dve", lambda e: e.reciprocal(out=rden[:], in_=tsum[:]), reads=["tsum"], writes=["rden"])
            if len(parts) == 1:
                P.op("dve", lambda e: e.tensor_tensor(out=tnum[:], in0=acc0, in1=rden[:], op=ALU.mult),
                     reads=bks + ["rden"], writes=["tnum"])
            else:
                P.op("dve", lambda e: e.tensor_copy(out=tnum[:], in_=acc0), reads=bks, writes=["tnum"])
                for (ac, _, U) in parts[1:]:
                    pv, sv = vw(ac, tnum, U)
                    P.op("dve", lambda e, pv=pv, sv=sv: e.tensor_tensor(out=sv, in0=pv, in1=sv, op=ALU.add),
                         reads=bks + ["tnum"], writes=["tnum"])
                P.op("dve", lambda e: e.tensor_tensor(out=tnum[:], in0=tnum[:], in1=rden[:], op=ALU.mult),
                     reads=["tnum", "rden"], writes=["tnum"])
            P.op("dve", lambda e: e.tensor_tensor(out=dest, in0=tnum[:], in1=z[:], op=ALU.mult),
                 reads=["tnum", ("zs", zi)], writes=[destkey])

        def phase_q(seg):
            core = SEG_CORE[seg]
            nt_ = core // TQ
            sc = float(1.0 / np.sqrt(128.0))
            scm = 1.0 / 16.0
            c1, c2, c3 = KTC[seg, "a", 0], KTC[seg, "a", 1], KTC[seg, "a", 2]
            V1, V2, V3 = VC[seg, "a", 0], VC[seg, "a", 1], VC[seg, "a", 2]
            ck = [("cache", seg, g) for g in ("n", "d4", "d16")]
            A0, A1, A2 = (PS[b] for b in BK_ACC)
            acc = [A0[:, 0:256], A0[:, 256:512], A1[:, 0:256]]
            den = [A1[:, 256:512], A2[:, 0:256], A2[:, 256:512]]
            acck = [("ps", BK_ACC[0]), ("ps", BK_ACC[0]), ("ps", BK_ACC[1])]
            denk = [("ps", BK_ACC[1]), ("ps", BK_ACC[2]), ("ps", BK_ACC[2])]

            def val(col, rows):
                return valw[0:rows, col:col + 1].to_broadcast([rows, 128])

            for t in range(min(nt_, DEBUG_QT)):
                tok0 = t * TQ
                load_g("gn")
                front_pipe([(X[seg], [(0, 128, MAXH + tok0 + sub * 128, 1)], sub, sub * 128) for sub in range(2)],
                           hT, hq_keys, gbank)
                P.dma("sp", lambda e, tok0=tok0: [e.dma_start(out=cosb[:, 0:TQ], in_=COSQ[seg][:, tok0:tok0 + TQ])],
                      "cos", writes=["cos"])
                P.dma("sp", lambda e, tok0=tok0: [e.dma_start(out=sinb[:, 0:TQ], in_=SINQ[seg][:, tok0:tok0 + TQ])],
                      "sin", writes=["sin"])
                P.dma("pool", lambda e, t=t: [e.dma_start(out=valw[:], in_=VALW[seg][t])], "valw", writes=["valw"])
                P.dma("pool", lambda e, t=t: [e.dma_start(out=maskt[:].rearrange("p m n -> p (m n)"), in_=MASKW[seg][t])],
                      "maskt", writes=["maskt"])

                def a_proj1(s):
                    hs = []
                    qs = []
                    for g in range(2):
                        bk = gemm_q(("qa", s, g))
                        hs.append(rope_a(bk, TQ))
                        if g >= 1:
                            qs.append(q_rope_finish(hs[g - 1]))
                    return hs, qs

                def a_proj2(s, part):
                    hs, qs = part
                    bk = gemm_q(("qa", s, 2))
                    hs.append(rope_a(bk, TQ))
                    qs.append(q_rope_finish(hs[1]))
                    bk = gemm_q(("za", s))
                    qs.append(q_rope_finish(hs[2]))
                    z = silu_z(bk)
                    return qs, z

                def a_win(s):
                    cs_ = slice(s * 128, (s + 1) * 128)
                    P.dma("sp", lambda e, s=s, tok0=tok0, cs_=cs_: [
                        e.dma_start(out=k1[:], in_=c1[s][:, tok0 + 64:tok0 + 448]),
                        e.dma_start(out=v1[:], in_=V1[tok0 + 64:tok0 + 448, cs_].rearrange("(c p) d -> p c d", p=128)),
                    ], "win1", reads=[ck[0]], writes=["k1", "v1"], n=2)
                    j4 = tok0 // 4
                    P.dma("sp", lambda e, s=s, j4=j4, cs_=cs_: [
                        e.dma_start(out=k2[:], in_=c2[s].rearrange("p (u n) -> p u n", u=4)[:, :, j4:j4 + 192]),
                        e.dma_start(out=v2lo[:], in_=V2.rearrange("(u n) d -> n u d", u=4)[j4:j4 + 128, :, cs_]),
                        e.dma_start(out=v2hi[0:64], in_=V2.rearrange("(u n) d -> n u d", u=4)[j4 + 128:j4 + 192, :, cs_]),
                    ], "win2", reads=[ck[1]], writes=["k2", "v2"], n=3)
                    j16 = tok0 // 16
                    P.dma("sp", lambda e, s=s, j16=j16, cs_=cs_: [
                        e.dma_start(out=k3[:], in_=c3[s].rearrange("p (u n) -> p u n", u=16)[:, :, j16:j16 + 144]),
                        e.dma_start(out=v3lo[:], in_=V3.rearrange("(u n) d -> n u d", u=16)[j16:j16 + 128, :, cs_]),
                        e.dma_start(out=v3hi[0:16], in_=V3.rearrange("(u n) d -> n u d", u=16)[j16 + 128:j16 + 144, :, cs_]),
                    ], "win3", reads=[ck[2]], writes=["k3", "v3"], n=3)

                def a_geom(g, qt):
                    if g == 0:
                        return dict(U=2, nq=128, klo=lambda u: k1[:, u * 128:(u + 1) * 128],
                                    khi=lambda u: k1[:, (u + 1) * 128:(u + 2) * 128],
                                    vlo=lambda u: v1[:, u, :], vhi=lambda u: v1[:, u + 1, :],
                                    vallo=lambda u: val(0 + u, 128), valhi=lambda u: val(1 + u, 128),
                                    qcol=(lambda u: qt[:, u * 128:(u + 1) * 128]) if qt is not None else None,
                                    rk=["k1", "v1"])
                    if g == 1:
                        return dict(U=4, nq=64, klo=lambda u: k2[:, u, 0:128], khi=lambda u: k2[:, u, 128:192],
                                    vlo=lambda u: v2lo[:, u, :], vhi=lambda u: v2hi[0:64, u, :],
                                    vallo=lambda u: val(3 + u, 128), valhi=lambda u: val(7 + u, 64),
                                    qcol=(lambda u: qt[:, :].rearrange("p (j u) -> p u j", u=4)[:, u, :]) if qt is not None else None,
                                    rk=["k2", "v2"])
                    return dict(U=16, nq=16, klo=lambda u: k3[:, u, 0:128], khi=lambda u: k3[:, u, 128:144],
                                vlo=lambda u: v3lo[:, u, :], vhi=lambda u: v3hi[0:16, u, :],
                                vallo=lambda u: val(11 + u, 128), valhi=lambda u: val(27 + u, 16),
                                qcol=(lambda u: qt[:, :].rearrange("p (j u) -> p u j", u=16)[:, u, :]) if qt is not None else None,
                                rk=["k3", "v3"])

                def a_scores(qs):
                    pts = []
                    for g in range(3):
                        qi, qt = qs[g]
                        G = a_geom(g, qt)
                        U, nq = G["U"], G["nq"]
                        mms = []
                        for u in range(U):
                            mms.append((0, u * nq, nq, 128, [G["klo"](u)], [G["qcol"](u)]))
                            mms.append((1, u * nq, nq, nq, [G["khi"](u)], [G["qcol"](u)]))
                        pts.append(score_exp(mms, 128, nq, sc, maskt[:, 2 * g, :], maskt[:, 2 * g + 1, :],
                                             [("qT", qi)] + G["rk"]))
                    return pts

                def a_pv(pts):
                    for g in range(3):
                        pi, PT = pts[g]
                        G = a_geom(g, None)
                        U, nq, rk = G["U"], G["nq"], G["rk"]
                        for u in range(U):
                            cs2 = slice(u * nq, (u + 1) * nq)
                            cs2h = slice(256 + u * nq, 256 + (u + 1) * nq)
                            vlo, vhi = G["vlo"](u), G["vhi"](u)
                            P.op("pe", lambda e, g=g, cs2=cs2, vlo=vlo, PT=PT: e.matmul(
                                acc[g][:, cs2], vlo, PT[:, cs2], start=True, stop=False),
                                reads=[("PT", pi)] + rk, writes=[acck[g]])
                            P.op("pe", lambda e, g=g, cs2=cs2, cs2h=cs2h, vhi=vhi, PT=PT, nq=nq: e.matmul(
                                acc[g][:, cs2], vhi, PT[0:nq, cs2h], start=False, stop=True),
                                reads=[("PT", pi)] + rk, writes=[acck[g]])
                        P.op("pe", lambda e, g=g, PT=PT: e.matmul(den[g], ones, PT[:, 0:256], start=True, stop=False),
                             reads=[("PT", pi), "cst"], writes=[denk[g]])
                        P.op("pe", lambda e, g=g, PT=PT, nq=nq: e.matmul(den[g], ones[0:nq, :], PT[0:nq, 256:512],
                                                                          start=False, stop=True),
                             reads=[("PT", pi), "cst"], writes=[denk[g]])

                st_ = a_proj2(0, a_proj1(0))
                a_win(0)
                for s in range(8):
                    qs, (zi, z) = st_
                    if s < 7:
                        part = a_proj1(s + 1)
                    pts = a_scores(qs)
                    if s < 7:
                        nxt = a_proj2(s + 1, part)
                    a_pv(pts)
                    if s < 7:
                        a_win(s + 1)
                    parts = [(acc[0], den[0], None), (acc[1], den[1], 4), (acc[2], den[2], 16)]
                    finish_head(parts, None, z, zi, bzT[:, s, :], R(KC + s))
                    if s < 7:
                        st_ = nxt

                def b_win(j):
                    P.dma("sp", lambda e, j=j, tok0=tok0: [
                        e.dma_start(out=kbw[:], in_=KTC[seg, "b", 0][j][:, tok0:tok0 + 512]),
                        e.dma_start(out=vbw[:], in_=VC[seg, "b", 0][tok0:tok0 + 512, j * 128:(j + 1) * 128]
                                    .rearrange("(c p) d -> p c d", p=128)),
                    ], "winb", reads=[("cache", seg, "n")], writes=["kb", "vb"], n=2)

                def b_q(h):
                    bk = gemm_q(("qb", h))
                    return rope_a(bk, TQ)

                qcur = []
                hnd = None
                for hq in range(4):
                    h2 = b_q(hq)
                    if hnd is not None:
                        qcur.append(q_rope_finish(hnd))
                    hnd = h2
                qcur.append(q_rope_finish(hnd))
                b_win(0)
                A0b, A1b = PS[BK_ACC[0]], PS[BK_ACC[1]]
                def b_scores(h, qi, qt):
                    mm_pn, mm_c = [], []
                    for sb in range(2):
                        qc = qt[:, sb * 128:(sb + 1) * 128]
                        mm_pn.append((0, sb * 128, 128, 128, [kbw[:, sb * 128:(sb + 1) * 128]], [qc]))
                        mm_pn.append((1, sb * 128, 128, 128, [kbw[:, (sb + 2) * 128:(sb + 3) * 128]], [qc]))
                        mm_c.append((0, sb * 128, 128, 128, [kbw[:, (sb + 1) * 128:(sb + 2) * 128]], [qc]))
                    a_ = score_exp(mm_pn, 128, 128, sc, masks[:, 0, :], masks[:, 1, :], [("qT", qi), "kb"])
                    b_ = score_exp(mm_c, 128, 0, sc, None, None, [("qT", qi), "kb"])
                    return a_, b_

                def b_pv(h, sc_, z, zi):
                    (p1, PT1), (p2, PT2) = sc_
                    for sb in range(2):
                        cs2 = slice(sb * 128, (sb + 1) * 128)
                        cs2h = slice(256 + sb * 128, 256 + (sb + 1) * 128)
                        seq = [(vbw[:, sb, :], val(43 + sb, 128), PT1[:, cs2], ("PT", p1)),
                               (vbw[:, sb + 1, :], val(44 + sb, 128), PT2[:, cs2], ("PT", p2)),
                               (vbw[:, sb + 2, :], val(45 + sb, 128), PT1[:, cs2h], ("PT", p1))]
                        for n_, (vv, vl, pp, pk) in enumerate(seq):
                            P.op("pe", lambda e, vv=vv, pp=pp, cs2=cs2, n_=n_: e.matmul(
                                A0b[:, cs2], vv, pp, start=(n_ == 0), stop=(n_ == 2)),
                                reads=[pk, "vb"], writes=[("ps", BK_ACC[0])])
                            P.op("pe", lambda e, vl=vl, pp=pp, cs2=cs2, n_=n_: e.matmul(
                                A1b[:, cs2], vl, pp, start=(n_ == 0), stop=(n_ == 2)),
                                reads=[pk, "valw"], writes=[("ps", BK_ACC[1])])
                    finish_head([(A0b[:, 0:256], A1b[:, 0:256], None)], esink[:, h:h + 1], z, zi,
                                bzT[:, 8 + h, :], R(KC + 8 + h))

                for j in range(4):
                    qnext = []
                    pend = None
                    pend_q = None
                    for hq in range(4):
                        h = 4 * j + hq
                        qi, qt = qcur[hq]
                        sc_ = b_scores(h, qi, qt)
                        if pend_q is not None:
                            qnext.append(q_rope_finish(pend_q))
                            pend_q = None
                        if pend is not None:
                            b_pv(*pend)
                        bkz = gemm_q(("zb", h))
                        zi, z = silu_z(bkz)
                        if j < 3:
                            pend_q = b_q(4 * (j + 1) + hq)
                        pend = (h, sc_, z, zi)
                    if pend_q is not None:
                        qnext.append(q_rope_finish(pend_q))
                    b_pv(*pend)
                    if j < 3:
                        b_win(j + 1)
                        qcur = qnext

                for h in range(4):
                    mq = []
                    for c in range(2):
                        bk = gemm_q(("mq", h, c))
                        mq.append(evac_q(bk))
                    mms = []
                    for kc2 in range(2):
                        mms.append((kc2, 0, 256, 128,
                                    [KmT[:, h * 2 + c, kc2 * 128:(kc2 + 1) * 128] for c in range(2)],
                                    [mq[c][1][:, :] for c in range(2)]))
                    pi, PT = score_exp(mms, 128, 128, scm, None, None, [("qT", mq[0][0]), ("qT", mq[1][0]), "KmT"])
                    zz = []
                    for c in range(2):
                        bk = gemm_q(("zm", h, c))
                        zz.append(silu_z(bk))
                    for kc2 in range(2):
                        P.op("pe", lambda e, kc2=kc2, PT=PT: e.matmul(A1b[:, 0:256], ones, PT[:, kc2 * 256:(kc2 + 1) * 256],
                                                                        start=(kc2 == 0), stop=(kc2 == 1)),
                             reads=[("PT", pi), "cst"], writes=[("ps", BK_ACC[1])])
                    for c in range(2):
                        for kc2 in range(2):
                            P.op("pe", lambda e, kc2=kc2, c=c, PT=PT, h=h: e.matmul(
                                A0b[:, c * 256:(c + 1) * 256], Vm[:, kc2, h * 256 + c * 128:h * 256 + (c + 1) * 128],
                                PT[:, kc2 * 256:(kc2 + 1) * 256], start=(kc2 == 0), stop=(kc2 == 1)),
                                reads=[("PT", pi), "Vm"], writes=[("ps", BK_ACC[0])])
                    for c in range(2):
                        zi, z = zz[c]
                        finish_head([(A0b[:, c * 256:(c + 1) * 256], A1b[:, 0:256], None)], None, z, zi,
                                    bzT[:, 24 + 2 * h + c, :], R(KC + 24 + 2 * h + c))

                bzk = [R(KC + i) for i in range(MIXC)]
                br_rows = [(0, 8), (8, 24), (24, 32)]
                for dc in range(KC):
                    for bi, nm in enumerate(("ga", "gb", "gm")):
                        bk = gemm_q((nm, dc))
                        P.op("act", lambda e, bk=bk, bi=bi: e.activation(out=gs[bi][:], in_=PS[bk][:, 0:TQ],
                                                                         func=AF.Sigmoid),
                             reads=[("ps", bk)], writes=[("gs", bi)])
                    wb, wbk = get_w("wbr", dc)
                    for bi in range(3):
                        gsb = gs[bi]
                        bk2 = gbank()
                        r0, r1 = br_rows[bi]
                        for rc in range(r0, r1):
                            P.op("pe", lambda e, bk2=bk2, rc=rc, wb=wb, r0=r0, r1=r1: e.matmul(
                                PS[bk2][:, 0:TQ], wb[:, rc, :], bzT[:, rc, :],
                                start=(rc == r0), stop=(rc == r1 - 1)),
                                reads=[wbk, bzk[rc]], writes=[("ps", bk2)])
                        if bi == 0:
                            P.op("dve", lambda e, bk2=bk2, gsb=gsb: e.tensor_tensor(
                                out=uacc[:], in0=PS[bk2][:, 0:TQ], in1=gsb[:], op=ALU.mult),
                                reads=[("ps", bk2), ("gs", bi)], writes=["tsum"])
                        else:
                            P.op("dve", lambda e, bk2=bk2, gsb=gsb: e.tensor_tensor(
                                out=utmp[:], in0=PS[bk2][:, 0:TQ], in1=gsb[:], op=ALU.mult),
                                reads=[("ps", bk2), ("gs", bi)], writes=["tnum"])
                            if bi == 1:
                                P.op("dve", lambda e: e.tensor_tensor(out=uacc[:], in0=uacc[:], in1=utmp[:], op=ALU.add),
                                     reads=["tsum", "tnum"], writes=["tsum"])
                            else:
                                P.op("dve", lambda e, dc=dc: e.tensor_tensor(
                                    out=uT[:, dc, :], in0=uacc[:], in1=utmp[:], op=ALU.add),
                                    reads=["tsum", "tnum"], writes=[R(KC + MIXC + dc)])

                uk = [R(KC + MIXC + i) for i in range(KC)]
                for sub in range(2):
                    P.dma("sp", lambda e, sub=sub, tok0=tok0: [
                        e.dma_start(out=xo[sub][:], in_=X[seg][MAXH + tok0 + sub * 128:MAXH + tok0 + (sub + 1) * 128, :])],
                        ("xo", sub), writes=[("xo", sub)])
                load_g("gf")
                for ob in range(KC // 2):
                    w4, wks = get_w_pair("wout", 2 * ob)
                    for sub in range(2):
                        bk = gbank()
                        pv = PS[bk][:, 0:256].rearrange("p (s c) -> p s c", s=2)
                        for kc in range(KC):
                            P.op("pe", lambda e, pv=pv, kc=kc, sub=sub, w4=w4: e.matmul(
                                pv, uT[:, kc, sub * 128:(sub + 1) * 128], w4[:, :, kc, :],
                                start=(kc == 0), stop=(kc == KC - 1)), reads=wks + [uk[kc]], writes=[("ps", bk)])
                        P.op("dve", lambda e, bk=bk, sub=sub, ob=ob: e.tensor_tensor(
                            out=xo[sub][:, ob * 256:(ob + 1) * 256], in0=PS[bk][:, 0:256],
                            in1=xo[sub][:, ob * 256:(ob + 1) * 256], op=ALU.add),
                            reads=[("ps", bk), ("xo", sub)], writes=[("xo", sub)])
                for sub in range(2):
                    xb = xo[sub]
                    c0 = 4 * sub
                    hbj = hbs[sub]
                    sk2 = ("stat2", sub)
                    P.op("act", lambda e, xb=xb, c0=c0, hbj=hbj: e.activation(out=hbj[:], in_=xb[:], func=AF.Square,
                                                                              accum_out=stat2[:, c0:c0 + 1]),
                         reads=[("xo", sub)], writes=[("hb", sub), sk2])
                    P.op("dve", lambda e, c0=c0: e.tensor_scalar(out=stat2[:, c0 + 1:c0 + 2], in0=stat2[:, c0:c0 + 1],
                                                                 scalar1=1.0 / D, scalar2=1e-6, op0=ALU.mult, op1=ALU.add),
                         reads=[sk2], writes=[sk2])
                    P.op("act", lambda e, c0=c0: e.activation(out=stat2[:, c0 + 2:c0 + 3], in_=stat2[:, c0 + 1:c0 + 2],
                                                              func=AF.Sqrt), reads=[sk2], writes=[sk2])
                    P.op("dve", lambda e, c0=c0: e.reciprocal(out=stat2[:, c0 + 3:c0 + 4], in_=stat2[:, c0 + 2:c0 + 3]),
                         reads=[sk2], writes=[sk2])
                    P.op("dve", lambda e, xb=xb, c0=c0: e.scalar_tensor_tensor(
                        out=xb[:], in0=xb[:], scalar=stat2[:, c0 + 3:c0 + 4], in1=gt[:], op0=ALU.mult, op1=ALU.mult),
                        reads=[("xo", sub), sk2, "gt"], writes=[("xo", sub)])
                    P.dma("sp", lambda e, xb=xb, sub=sub, tok0=tok0: [
                        e.dma_start(out=Y[seg][tok0 + sub * 128:tok0 + (sub + 1) * 128, :], in_=xb[:])],
                        ("xo", sub), reads=[("xo", sub)], writes=[("y", seg, t, sub)], final=True)

        def whole():
            wstate["n"] = 0
            for k in cnt:
                cnt[k] = 0
            for k in rr:
                rr[k] = 0
            load_consts()
            for seg in "PS":
                for grp in ("n", "d4", "d16"):
                    phase_kv(seg, grp)
            if P.dry:
                wstate["qstart"] = wstate["n"]
            for seg in "PS":
                phase_m(seg)
                phase_q(seg)

        P.dry = True
        whole()
        P.dry = False
        build_sched()
        cw_keys.clear()
        whole()
        P.emit()
        info = dict(n_ops=P.n_ops, n_wait=P.n_wait, sbuf_left=nc.sbuf_bytes_remaining, nplan=len(plan))
    return nc, info


def rope_tables(pos):
    inv = (10000.0 ** (-np.arange(0, HD, 2, dtype=np.float32) / HD)).astype(np.float32)
    ang = pos.astype(np.float32)[None, :] * inv[:, None]
    c = np.cos(ang).astype(np.float32)
    s = np.sin(ang).astype(np.float32)
    return np.concatenate([c, c], 0), np.concatenate([s, s], 0)


def const_table():
    ident = np.eye(128, dtype=np.float32)
    rotm = np.zeros((128, 128), np.float32)
    for m in range(64):
        rotm[m + 64, m] = -1.0
        rotm[m, m + 64] = 1.0
    ones = np.ones((128, 128), np.float32)
    kl = np.arange(128)[:, None]
    msk = []
    for nq, U in ((128, 2),):
        ql = np.arange(nq)[None, :]
        lo = (kl >= ql).astype(np.float32)
        hi = ((kl <= ql) & (kl < nq)).astype(np.float32)
        msk.append(np.tile(lo, (1, U)))
        msk.append(np.tile(hi, (1, U)))
    return np.concatenate([ident, rotm, ones] + msk, axis=1).astype(np.float32)


def tile_masks(vw_t):
    kl = np.arange(128)[:, None]
    out = []
    for (nq, U, lo0, hi0) in ((128, 2, 0, 1), (64, 4, 3, 7), (16, 16, 11, 27)):
        ql = np.arange(nq)[None, :]
        lo = (kl >= ql).astype(np.float32)
        hi = ((kl <= ql) & (kl < nq)).astype(np.float32)
        out.append(np.concatenate([lo * vw_t[:, lo0 + u:lo0 + u + 1] for u in range(U)], axis=1))
        out.append(np.concatenate([hi * vw_t[:, hi0 + u:hi0 + u + 1] for u in range(U)], axis=1))
    return np.concatenate(out, axis=1).astype(np.float32)


def block_layout(w, cols, KC):
    sub = w[:, cols]
    return np.ascontiguousarray(sub.reshape(KC, 128, len(cols)).transpose(1, 0, 2)).reshape(128, KC * len(cols))


def host_prepare(inputs, D):
    KC = D // 128
    NDB = D // 256
    w_in = np.asarray(inputs["w_in"][0], np.float32)
    w_mem = np.asarray(inputs["w_mem_kv"][0], np.float32)
    w_br = np.asarray(inputs["w_branch"][0], np.float32)
    w_out = np.asarray(inputs["w_out"][0], np.float32)
    blocks = in_col_blocks(D)
    win = np.stack([block_layout(w_in, c, KC) for c in blocks.values()])
    wmem = np.stack([block_layout(w_mem, np.arange(b * 128, (b + 1) * 128), KC) for b in range(16)])
    wbr = np.stack([block_layout(w_br, np.arange(b * 128, (b + 1) * 128), MIXC) for b in range(KC)])
    wout = np.stack([block_layout(w_out, np.arange(b * 128, (b + 1) * 128), KC) for b in range(KC)])
    cst = const_table()
    shared = dict(win=win, wmem=wmem, wbr=wbr, wout=wout, const=cst,
                  gn=np.asarray(inputs["g_norm"], np.float32).reshape(1, D),
                  gm=np.asarray(inputs["g_mem"], np.float32).reshape(1, D),
                  gf=np.asarray(inputs["g_final"], np.float32).reshape(1, D),
                  sink=np.asarray(inputs["attn_sink"], np.float32).reshape(1, 16))
    xs = {"P": np.asarray(inputs["x_prompt"], np.float32), "S": np.asarray(inputs["x_sample"], np.float32)}
    mems = {"P": np.asarray(inputs["mem_prompt"], np.float32), "S": np.asarray(inputs["mem_sample"], np.float32)}
    in_maps = []
    for c in range(NCORES):
        m = dict(shared)
        b = c // 4
        ch = c % 4
        for s in "PS":
            core = SEG_CORE[s]
            Ls = SEG_LEN[s]
            a = ch * core
            xe = np.zeros((core + 2 * MAXH + 16, D), np.float32)
            lo = max(0, a - MAXH)
            hi = min(Ls, a + core + MAXH)
            xe[lo - (a - MAXH):hi - (a - MAXH)] = xs[s][b, lo:hi]
            m["x" + s] = xe
            m["mem" + s] = np.ascontiguousarray(mems[s][b])
            cq, sq = rope_tables(np.arange(a, a + core))
            m["cosq" + s] = cq
            m["sinq" + s] = sq
            for g, (d, H) in GRP.items():
                L = core + 2 * H
                npc = L // d
                idx = np.arange(L)
                r, j = idx // npc, idx % npc
                pos = a - H + r + d * j
                ck, sk = rope_tables(np.clip(pos, 0, Ls - 1))
                m["cosk%s%s" % (s, g)] = ck
                m["sink%s%s" % (s, g)] = sk
            nt = core // TQ
            vw = np.zeros((nt, 128, NVAL), np.float32)
            p = np.arange(128)
            for t in range(nt):
                t0 = a + t * TQ

                def ok(pos):
                    return ((pos >= 0) & (pos < Ls)).astype(np.float32)

                for cc in range(3):
                    vw[t, :, cc] = ok(t0 - 64 + cc * 128 + p)
                for u in range(4):
                    vw[t, :, 3 + u] = ok(t0 + u + 4 * (p - 64))
                    vw[t, :, 7 + u] = ok(t0 + u + 4 * (p + 64))
                for u in range(16):
                    vw[t, :, 11 + u] = ok(t0 + u + 16 * (p - 64))
                    vw[t, :, 27 + u] = ok(t0 + u + 16 * (p + 64))
                for cc in range(4):
                    vw[t, :, 43 + cc] = ok(t0 - 128 + cc * 128 + p)
            m["valw" + s] = vw
            m["maskw" + s] = np.stack([tile_masks(vw[t]) for t in range(nt)])
        in_maps.append(m)
    return in_maps


_CACHE = {}


def run(inputs, D, trace=False):
    if D not in _CACHE:
        _CACHE[D] = build_program(D)
    nc, info = _CACHE[D]
    in_maps = host_prepare(inputs, D)
    res = run_bass_kernel_spmd(nc, in_maps, core_ids=list(range(NCORES)))
    B = 2
    yp = np.zeros((B, SEG_LEN["P"], D), np.float32)
    ys = np.zeros((B, SEG_LEN["S"], D), np.float32)
    for c in range(NCORES):
        b, ch = c // 4, c % 4
        r = res.results[c]
        yp[b, ch * 1024:(ch + 1) * 1024] = r["yP"]
        ys[b, ch * 2048:(ch + 1) * 2048] = r["yS"]
    return yp, ys


def kernel(x_prompt, x_sample, mem_prompt, mem_sample, g_norm, w_in, attn_sink, g_mem, w_mem_kv, w_branch, w_out,
           g_final):
    D = int(np.asarray(x_prompt).shape[-1])
    inputs = dict(x_prompt=x_prompt, x_sample=x_sample, mem_prompt=mem_prompt, mem_sample=mem_sample, g_norm=g_norm,
                  w_in=w_in, attn_sink=attn_sink, g_mem=g_mem, w_mem_kv=w_mem_kv, w_branch=w_branch, w_out=w_out,
                  g_final=g_final)
    return run(inputs, D)
```
